# Optimizing a Trainium2 kernel written in Bass

```python
import jax, jax.numpy as jnp
from jax import lax
import numpy as np

D_MODEL = 2048
BATCH = 32
SEQ = 256
DEPTH = 1
DEC_BATCH = 8
DEC_SEQ = 1024
PAST_LEN = 256

GRID_W = 64
HEAD_DIM = 128
N_HEADS_TOTAL = D_MODEL // HEAD_DIM
N_HEADS_A = N_HEADS_TOTAL // 2
N_HEADS_B = N_HEADS_TOTAL - N_HEADS_A
N_KV_B = max(1, N_HEADS_B // 4)
MIX_WIDTH = N_HEADS_TOTAL * HEAD_DIM
WIN_H = 8
WIN_W = 16
D_FF = 5632
ROPE_THETA = 10000.0
EPS = 1e-6
Q_BLOCK = 128
NEG_INF = -1e30
A_WIDTH = N_HEADS_A * HEAD_DIM
B_Q_WIDTH = N_HEADS_B * HEAD_DIM
B_KV_WIDTH = N_KV_B * HEAD_DIM
IN_COLS = 3 * A_WIDTH + B_Q_WIDTH + 2 * B_KV_WIDTH

kernel_name = "hybrid_natten_gqa_dit_step"


def rms_norm(x, g):
    xf = x.astype(jnp.float32)
    y = xf * lax.rsqrt(jnp.mean(xf * xf, axis=-1, keepdims=True) + EPS)
    return (y * g.astype(jnp.float32)).astype(x.dtype)


def modulation(cond, w_mod, b_mod):
    m = jax.nn.silu(cond) @ w_mod + b_mod
    return jnp.split(m, 6, axis=-1)


def split_heads(x, n):
    b, l, _ = x.shape
    return x.reshape(b, l, n, HEAD_DIM).transpose(0, 2, 1, 3)


def merge_heads(x):
    b, n, l, d = x.shape
    return x.transpose(0, 2, 1, 3).reshape(b, l, n * d)


def project(h, w_in):
    qkv = h @ w_in
    qa, ka, va, qb, kb, vb = jnp.split(
        qkv, [A_WIDTH, 2 * A_WIDTH, 3 * A_WIDTH, 3 * A_WIDTH + B_Q_WIDTH,
              3 * A_WIDTH + B_Q_WIDTH + B_KV_WIDTH], axis=-1)
    return (split_heads(qa, N_HEADS_A), split_heads(ka, N_HEADS_A), split_heads(va, N_HEADS_A),
            split_heads(qb, N_HEADS_B), split_heads(kb, N_KV_B), split_heads(vb, N_KV_B))


def rope_2d(x):
    l = x.shape[2]
    t = jnp.arange(l)
    half = HEAD_DIM // 2
    freqs = ROPE_THETA ** (-jnp.arange(0, half, 2, dtype=jnp.float32) / half)
    xf = x.astype(jnp.float32)

    def rot(xh, pos):
        ang = pos.astype(jnp.float32)[:, None] * freqs[None, :]
        cos, sin = jnp.cos(ang), jnp.sin(ang)
        x1, x2 = xh[..., :half // 2], xh[..., half // 2:]
        return jnp.concatenate([x1 * cos - x2 * sin, x2 * cos + x1 * sin], axis=-1)

    out = jnp.concatenate([rot(xf[..., :half], t // GRID_W), rot(xf[..., half:], t % GRID_W)], axis=-1)
    return out.astype(x.dtype)


def attend(q, k, v):
    b, hq, lq, dh = q.shape
    hkv = k.shape[1]
    g = hq // hkv
    nblk = lq // Q_BLOCK
    scale = dh ** -0.5
    qb = q.reshape(b, hkv, g, nblk, Q_BLOCK, dh).transpose(3, 0, 1, 2, 4, 5)

    def block(qi):
        s = jnp.einsum('bkgqd,bkld->bkgql', qi, k).astype(jnp.float32) * scale
        p = jax.nn.softmax(s, axis=-1).astype(v.dtype)
        return jnp.einsum('bkgql,bkld->bkgqd', p, v)

    o = lax.map(block, qb)
    return o.transpose(1, 2, 3, 0, 4, 5).reshape(b, hq, lq, dh)


def neighbourhood_attention(q, k, v, k_ctx, v_ctx, rpb):
    b, h, l, dh = q.shape
    rows = l // GRID_W
    kh = min(WIN_H, rows)
    scale = dh ** -0.5
    r = jnp.arange(rows)
    rs = jnp.clip(r - kh // 2, 0, rows - kh)
    row_idx = rs[:, None] + jnp.arange(kh)[None, :]
    col = jnp.arange(GRID_W)
    cs = jnp.clip(col - WIN_W // 2, 0, GRID_W - WIN_W)
    col_valid = (col[None, :] >= cs[:, None]) & (col[None, :] < cs[:, None] + WIN_W)
    qg = q.reshape(b, h, rows, GRID_W, dh)
    kg = k.reshape(b, h, rows, GRID_W, dh)
    vg = v.reshape(b, h, rows, GRID_W, dh)
    k_rows = kg[:, :, row_idx]
    v_rows = vg[:, :, row_idx]
    roff = row_idx - r[:, None] + WIN_H - 1
    coff = jnp.clip(col[None, :] - col[:, None] + WIN_W - 1, 0, 2 * WIN_W - 2)
    bias = rpb[:, roff[:, None, :, None], coff[None, :, None, :]]
    s_nb = jnp.einsum('bhrqd,bhrikd->bhrqik', qg, k_rows).astype(jnp.float32) * scale
    s_nb = s_nb + bias.astype(jnp.float32)
    s_nb = jnp.where(col_valid[:, None, :], s_nb, NEG_INF)
    s_nb = s_nb.reshape(b, h, rows, GRID_W, kh * GRID_W)
    s_ctx = jnp.einsum('bhrqd,bhcd->bhrqc', qg, k_ctx).astype(jnp.float32) * scale
    p = jax.nn.softmax(jnp.concatenate([s_nb, s_ctx], axis=-1), axis=-1).astype(v.dtype)
    p_nb = p[..., :kh * GRID_W].reshape(b, h, rows, GRID_W, kh, GRID_W)
    p_ctx = p[..., kh * GRID_W:]
    o = (jnp.einsum('bhrqik,bhrikd->bhrqd', p_nb, v_rows)
         + jnp.einsum('bhrqc,bhcd->bhrqd', p_ctx, v_ctx))
    return o.reshape(b, h, l, dh)


def conv_ffn(h, w_up, conv_w, conv_b, w_down):
    u = h @ w_up
    up = jnp.pad(u, ((0, 0), (1, 1), (0, 0)))
    u = up[:, :-2] * conv_w[0] + up[:, 1:-1] * conv_w[1] + up[:, 2:] * conv_w[2] + conv_b
    val, gate = jnp.split(u, 2, axis=-1)
    return (jax.nn.silu(gate) * val) @ w_down


def setup_inputs(seed: int = 0) -> dict:
    key = jax.random.key(seed)
    ks = jax.random.split(key, 24)
    f32 = jnp.float32
    nrm = lambda k, s, sc: jax.random.normal(k, s, f32) * sc
    return {
        "x_prompt": nrm(ks[0], (BATCH, SEQ, D_MODEL), 1.0),
        "x_sample": nrm(ks[1], (DEC_BATCH, DEC_SEQ, D_MODEL), 1.0),
        "c": nrm(ks[2], (DEC_BATCH, D_MODEL), 1.0),
        "cache_a_k": nrm(ks[3], (DEC_BATCH, DEPTH, N_HEADS_A, PAST_LEN, HEAD_DIM), 1.0),
        "cache_a_v": nrm(ks[4], (DEC_BATCH, DEPTH, N_HEADS_A, PAST_LEN, HEAD_DIM), 1.0),
        "cache_b_k": nrm(ks[5], (DEC_BATCH, DEPTH, N_KV_B, PAST_LEN, HEAD_DIM), 1.0),
        "cache_b_v": nrm(ks[6], (DEC_BATCH, DEPTH, N_KV_B, PAST_LEN, HEAD_DIM), 1.0),
        "c_ctx": nrm(ks[7], (D_MODEL,), 1.0),
        "w_mod": nrm(ks[8], (DEPTH, D_MODEL, 6 * D_MODEL), D_MODEL ** -0.5),
        "b_mod": nrm(ks[9], (DEPTH, 6 * D_MODEL), 0.01),
        "g_attn_pre": 1.0 + nrm(ks[10], (DEPTH, D_MODEL), 0.01),
        "g_attn_post": 1.0 + nrm(ks[11], (DEPTH, D_MODEL), 0.01),
        "g_ffn_pre": 1.0 + nrm(ks[12], (DEPTH, D_MODEL), 0.01),
        "g_ffn_post": 1.0 + nrm(ks[13], (DEPTH, D_MODEL), 0.01),
        "w_in": nrm(ks[14], (DEPTH, D_MODEL, IN_COLS), D_MODEL ** -0.5),
        "rpb": nrm(ks[15], (DEPTH, N_HEADS_A, 2 * WIN_H - 1, 2 * WIN_W - 1), 0.1),
        "g_qnorm": 1.0 + nrm(ks[16], (DEPTH, HEAD_DIM), 0.01),
        "g_knorm": 1.0 + nrm(ks[17], (DEPTH, HEAD_DIM), 0.01),
        "w_out": nrm(ks[18], (DEPTH, MIX_WIDTH, D_MODEL), MIX_WIDTH ** -0.5),
        "w_up": nrm(ks[19], (DEPTH, D_MODEL, 2 * D_FF), D_MODEL ** -0.5),
        "conv_w": nrm(ks[20], (DEPTH, 3, 2 * D_FF), 3 ** -0.5),
        "conv_b": nrm(ks[21], (DEPTH, 2 * D_FF), 0.01),
        "w_down": nrm(ks[22], (DEPTH, D_FF, D_MODEL), D_FF ** -0.5),
    }


def reference(x_prompt, x_sample, c, cache_a_k, cache_a_v, cache_b_k, cache_b_v, c_ctx,
              w_mod, b_mod, g_attn_pre, g_attn_post, g_ffn_pre, g_ffn_post, w_in, rpb,
              g_qnorm, g_knorm, w_out, w_up, conv_w, conv_b, w_down):
    xp = x_prompt
    xs = x_sample
    st_ak, st_av, st_bk, st_bv = [], [], [], []
    for l in range(DEPTH):
        sh1, sc1, ga1, sh2, sc2, ga2 = modulation(c_ctx[None, :], w_mod[l], b_mod[l])
        h = rms_norm(xp, g_attn_pre[l]) * (1 + sc1) + sh1
        qa, ka, va, qb, kb, vb = project(h, w_in[l])
        qb = rms_norm(qb, g_qnorm[l])
        kb = rms_norm(kb, g_knorm[l])
        oa = attend(qa, ka, va)
        ob = attend(qb, kb, vb)
        o = jnp.concatenate([merge_heads(oa), merge_heads(ob)], axis=-1) @ w_out[l]
        xp = xp + ga1 * rms_norm(o, g_attn_post[l])
        h = rms_norm(xp, g_ffn_pre[l]) * (1 + sc2) + sh2
        f = conv_ffn(h, w_up[l], conv_w[l], conv_b[l], w_down[l])
        xp = xp + ga2 * rms_norm(f, g_ffn_post[l])
        st_ak.append(ka)
        st_av.append(va)
        st_bk.append(kb)
        st_bv.append(vb)

        sh1, sc1, ga1, sh2, sc2, ga2 = modulation(c[:, None, :], w_mod[l], b_mod[l])
        h = rms_norm(xs, g_attn_pre[l]) * (1 + sc1) + sh1
        qa, ka, va, qb, kb, vb = project(h, w_in[l])
        oa = neighbourhood_attention(qa, ka, va, cache_a_k[:, l], cache_a_v[:, l], rpb[l])
        qb = rope_2d(rms_norm(qb, g_qnorm[l]))
        kb = rope_2d(rms_norm(kb, g_knorm[l]))
        kb_all = jnp.concatenate([cache_b_k[:, l], kb], axis=2)
        vb_all = jnp.concatenate([cache_b_v[:, l], vb], axis=2)
        ob = attend(qb, kb_all, vb_all)
        o = jnp.concatenate([merge_heads(oa), merge_heads(ob)], axis=-1) @ w_out[l]
        xs = xs + ga1 * rms_norm(o, g_attn_post[l])
        h = rms_norm(xs, g_ffn_pre[l]) * (1 + sc2) + sh2
        f = conv_ffn(h, w_up[l], conv_w[l], conv_b[l], w_down[l])
        xs = xs + ga2 * rms_norm(f, g_ffn_post[l])

    state_a_k = jnp.stack(st_ak, axis=1)
    state_a_v = jnp.stack(st_av, axis=1)
    state_b_k = jnp.stack(st_bk, axis=1)
    state_b_v = jnp.stack(st_bv, axis=1)
    return (xp, xs, state_a_k, state_a_v, state_b_k, state_b_v)
```

```python
import numpy as np
from contextlib import ExitStack
import concourse.bass as bass
import concourse.mybir as mybir
from concourse.bass_utils import run_bass_kernel_spmd

F32 = mybir.dt.float32
BF16 = mybir.dt.bfloat16
AF = mybir.ActivationFunctionType
ALU = mybir.AluOpType
AX = mybir.AxisListType

D = 2048
T = 1024
NT = 8
DFF = 5632
EPS = 1e-6
SCALE = 128 ** -0.5
NEG = -30000.0
COMPUTE = ("pe", "act", "dve", "pool")


class Op:
    __slots__ = ("eng", "fn", "deps", "signal", "sem", "val", "dma", "pos")

    def __init__(self, eng, fn, dma):
        self.eng = eng
        self.fn = fn
        self.deps = []
        self.signal = False
        self.sem = None
        self.val = 0
        self.dma = dma


class Res:
    __slots__ = ("w", "readers")

    def __init__(self):
        self.w = None
        self.readers = {}


class Sched:
    def __init__(self):
        self.streams = {e: [] for e in ("pe", "act", "dve", "pool", "sp")}
        self.res = {}
        self.fence = {}
        self.dma_cnt = {}

    def _get(self, key):
        r = self.res.get(key)
        if r is None:
            r = Res()
            self.res[key] = r
            f = self.fence.get(key[0])
            if f:
                r.w = f
        return r

    @staticmethod
    def _pos(o):
        return o.val if o.dma else o.pos

    def retire(self, region):
        ops = list(self.fence.get(region) or [])
        for k in [k for k in self.res if k[0] == region]:
            r = self.res.pop(k)
            if r.w is not None:
                ops.extend(r.w if isinstance(r.w, list) else [r.w])
            ops.extend(r.readers.values())
        best = {}
        for o in ops:
            ch = ("d", o.sem) if o.dma else ("c", o.eng)
            b = best.get(ch)
            if b is None or self._pos(o) > self._pos(b):
                best[ch] = o
        self.fence[region] = list(best.values())

    def add(self, eng, fn, reads=(), writes=(), dma_sem=None):
        op = Op(eng, fn, dma_sem is not None)
        if op.dma:
            c = self.dma_cnt.get(dma_sem, 0) + 16
            self.dma_cnt[dma_sem] = c
            op.sem = dma_sem
            op.val = c
        deps = {}

        def need(d, war):
            if d is op:
                return
            if not d.dma and not op.dma and d.eng == eng and (eng == "pe" or (war and eng != "pool")):
                return
            if op.dma and d.dma and d.sem == op.sem and not war:
                return
            deps[id(d)] = d

        for k in reads:
            r = self._get(k)
            if r.w is not None:
                for d in (r.w if isinstance(r.w, list) else [r.w]):
                    need(d, False)
        for k in writes:
            r = self._get(k)
            if r.w is not None:
                for d in (r.w if isinstance(r.w, list) else [r.w]):
                    need(d, False)
            for d in r.readers.values():
                need(d, True)
        ch = ("d", dma_sem) if op.dma else eng
        for k in reads:
            self.res[k].readers[ch] = op
        for k in writes:
            r = self.res[k]
            r.w = op
            r.readers = {}
        op.deps = list(deps.values())
        for d in op.deps:
            d.signal = True
        st = self.streams[eng]
        op.pos = len(st)
        st.append(op)
        return op

    def emit_block(self, block, sems, dma_sems):
        for eng in COMPUTE:
            c = 0
            for op in self.streams[eng]:
                if not op.dma:
                    if op.signal:
                        c += 1
                        op.val = c
                    op.sem = eng
        S = self

        def run(eng, e):
            known = {}
            for op in S.streams[eng]:
                need = {}
                for d in op.deps:
                    key = ("d", d.sem) if d.dma else ("c", d.sem)
                    if need.get(key, (0, None))[0] < d.val:
                        need[key] = (d.val, d)
                for key, (v, d) in need.items():
                    if known.get(key, 0) < v:
                        e.wait_ge(dma_sems[d.sem] if d.dma else sems[d.sem], v)
                        known[key] = v
                ins = op.fn(e)
                if op.dma:
                    ins.then_inc(dma_sems[op.sem], 16)
                elif op.signal:
                    ins.then_inc(sems[eng], 1)

        @block.tensor
        def _(e):
            run("pe", e)

        @block.scalar
        def _(e):
            run("act", e)

        @block.vector
        def _(e):
            run("dve", e)

        @block.gpsimd
        def _(e):
            run("pool", e)

        @block.sync
        def _(e):
            run("sp", e)
            for k, c in S.dma_cnt.items():
                e.wait_ge(dma_sems[k], c)


def build_program(stop_after=None):
    nc = bass.Bass("TRN2", target_bir_lowering=False)

    def din(name, shape):
        return nc.dram_tensor(name, list(shape), F32, kind="ExternalInput")

    def dout(name, shape):
        return nc.dram_tensor(name, list(shape), F32, kind="ExternalOutput")

    xg = [din("xp", (T, D)).ap(), din("xs", (T, D)).ap()]
    cond = din("cond", (2, D))
    cak = din("cak", (8, 256, 128)).ap()
    cav = din("cav", (8, 256, 128)).ap()
    cbk = din("cbk", (2, 256, 128)).ap()
    cbv = din("cbv", (2, 256, 128)).ap()
    w_mod = din("w_mod", (D, 6 * D)).ap()
    b_mod = din("b_mod", (1, 6 * D))
    gvec = [din(n, (1, D)) for n in ("g_attn_pre", "g_attn_post", "g_ffn_pre", "g_ffn_post")]
    w_in = din("w_in", (D, 4608)).ap()
    rpbpad = din("rpbpad", (1, 120 * 31 + 128))
    g_qn = din("g_qnorm", (1, 128))
    g_kn = din("g_knorm", (1, 128))
    w_out = din("w_out", (D, D)).ap()
    w_up = din("w_up", (D, 2 * DFF)).ap()
    conv_w = din("conv_w", (3, 2 * DFF)).ap()
    conv_b = din("conv_b", (1, 2 * DFF)).ap()
    w_down = din("w_down", (DFF, D)).ap()
    ident_d = din("ident", (128, 128)).ap()
    ropeC_d = din("ropeC", (T, 128)).ap()
    ropeS_d = din("ropeS", (T, 128)).ap()
    mask_d = din("maskrev", (64, 64)).ap()

    yg = [dout("yp", (T, D)).ap(), dout("ys", (T, D)).ap()]
    sak = dout("sak", (4, 8, 256, 128)).ap()
    sav = dout("sav", (4, 8, 256, 128)).ap()
    sbk = dout("sbk", (4, 2, 256, 128)).ap()
    sbv = dout("sbv", (4, 2, 256, 128)).ap()
    modrows = nc.dram_tensor("modrows", [6, 2, D], F32, kind="Internal")

    S = Sched()
    dsem_keys = []

    def dk(k):
        if k not in dsem_keys:
            dsem_keys.append(k)
        return k

    def op(eng, method, reads, writes, **kw):
        return S.add(eng, lambda e, m=method, kw=kw: getattr(e, m)(**kw), reads, writes)

    def dma(eng, out, in_, reads, writes, sem):
        return S.add(eng, lambda e, o=out, i=in_: e.dma_start(out=o, in_=i), reads, writes, dma_sem=dk(sem))

    with ExitStack() as es:
        es.enter_context(nc.allow_non_contiguous_dma(reason="small strided param loads"))

        def sb(name, shape, dt):
            return es.enter_context(nc.sbuf_tensor(name, list(shape), dt))

        R2 = sb("R2", (128, 16384), BF16)
        R3 = sb("R3", (128, 3 * 8192), BF16)
        R4 = sb("R4", (128, 44032), BF16)
        RS = sb("RS", (128, 8192), BF16)
        GBX = sb("GBX", (128, 8192), BF16)
        identf = sb("identf", (128, 128), F32)
        identb = sb("identb", (128, 128), BF16)
        modcols = sb("modcols", (128, 4, 16), F32)
        taps = sb("taps", (128, 4, 88), F32)
        stats = sb("stats", (128, 256), F32)
        epst = sb("epst", (128, 1), F32)
        gq_bc = sb("gq_bc", (128, 128), F32)
        gk_bc = sb("gk_bc", (128, 128), F32)
        maskb = sb("maskb", (128, 64), BF16)
        onesb = sb("onesb", (128, 128), BF16)
        negB = sb("negB", (128, 2), F32)
        condT = sb("condT", (128, 2, 16), F32)
        scT = sb("scT", (128, 16, 2), BF16)
        stageT = sb("stageT", (88, 4, 128), F32)
        PS = es.enter_context(nc.psum_tensor("PS", [128, 4096], F32))
        PSB = PS[:, :].bitcast(BF16)

        def bank(b, n=512):
            return PS[:, b * 512:b * 512 + n]

        def pskey(b):
            return ("ps", b)

        hT = R2[:, :].rearrange("p (c t) -> p c t", c=16)
        slots = [R3[:, s * 8192:(s + 1) * 8192] for s in range(3)]
        xt = [RS[:, s * 4096:(s + 1) * 4096].bitcast(F32) for s in range(2)]
        GB = GBX[:, 0:4096].bitcast(F32)
        xn = [GBX[:, 4096 + s * 2048:4096 + (s + 1) * 2048] for s in range(2)]

        stat_i = [0]

        def newstat(n=1):
            i = stat_i[0]
            if i + n > 256:
                i = 0
            stat_i[0] = i + n
            return stats[:, i:i + n], [("st", j) for j in range(i, i + n)]

        wq = []

        def wq_add(dram_ap, shape, parts=None):
            wq.append((dram_ap, shape, parts))
            return len(wq) - 1

        wstate = {"issued": 0}

        def w_issue(n):
            while wstate["issued"] <= n and wstate["issued"] < len(wq):
                i = wstate["issued"]
                src, shape, parts = wq[i]
                s = i % 3
                a, b = shape
                dst = slots[s][:, 0:a * b].rearrange("p (a b) -> p a b", a=a)
                if parts is None:
                    dma("pool", dst, src, [], [("w", s)], "w%d" % s)
                else:
                    for (psrc, c0_, c1_) in parts:
                        dma("pool", dst[:, :, c0_:c1_], psrc, [], [("w", s)], "w%d" % s)
                wstate["issued"] += 1

        def w_get(n):
            assert n >= wstate.get("last", 0), ("weight blocks must be consumed in registration order", n, wstate.get("last"))
            wstate["last"] = n
            assert n < 3 or (n - 3) in wdone, ("slot still in use", n)
            w_issue(n)
            a, b = wq[n][1]
            return slots[n % 3][:, 0:a * b].rearrange("p (a b) -> p a b", a=a), ("w", n % 3)

        wdone = set()

        def w_done(n):
            wdone.add(n)
            while wstate["issued"] < len(wq) and (wstate["issued"] < 3 or (wstate["issued"] - 3) in wdone):
                w_issue(wstate["issued"])

        w_mod3 = w_mod.rearrange("(c p) n -> p c n", p=128)
        w_in3 = w_in.rearrange("(c p) n -> p c n", p=128)
        w_out3 = w_out.rearrange("(c p) n -> p c n", p=128)
        w_up3 = w_up.rearrange("(c p) n -> p c n", p=128)
        w_dn3 = w_down.rearrange("(k p) n -> p k n", p=128)
        EGROUPS = [(0, 2), (2, 2), (4, 2), (6, 2), (8, 2), (10, 1)]
        mod_blocks = {}
        for nb in range(8):
            mod_blocks[nb] = wq_add(w_mod3[:, :, nb * 512:(nb + 1) * 512], (16, 512))
        WIDX = {}
        for g_ in range(2):
            WIDX[("in", g_)] = []
            for j in range(9):
                WIDX[("in", g_)].append(wq_add(w_in3[:, :, j * 512:(j + 1) * 512], (16, 512)))
                if g_ == 0:
                    nb = 8 + j
                    mod_blocks[nb] = wq_add(w_mod3[:, :, nb * 512:(nb + 1) * 512], (16, 512))
            if g_ == 0:
                for nb in range(17, 24):
                    mod_blocks[nb] = wq_add(w_mod3[:, :, nb * 512:(nb + 1) * 512], (16, 512))
            WIDX[("out", g_)] = [wq_add(w_out3[:, :, j * 512:(j + 1) * 512], (16, 512)) for j in range(4)]
            for qd_ in range(4):
                for (c0_, ncz_) in EGROUPS:
                    i0_ = 11 * qd_ + c0_
                    WIDX[("up", g_, qd_, c0_)] = wq_add(None, (16, 512), parts=[
                        (w_up3[:, :, i0_ * 128:(i0_ + ncz_) * 128], 0, ncz_ * 128),
                        (w_up3[:, :, DFF + i0_ * 128:DFF + (i0_ + ncz_) * 128], 256, 256 + ncz_ * 128)])
                WIDX[("dn", g_, qd_)] = [wq_add(w_dn3[:, 11 * qd_:11 * qd_ + 11, j * 512:(j + 1) * 512], (11, 512)) for j in range(4)]

        dma("sp", identf[:], ident_d, [], [("c", "identf")], "c0")
        op("dve", "tensor_copy", [("c", "identf")], [("c", "identb")], out=identb[:], in_=identf[:])
        op("dve", "memset", [], [("c", "eps")], ap=epst[:], constant=EPS)
        for i, (g_d, g_t) in enumerate(((g_qn, gq_bc), (g_kn, gk_bc))):
            dma("sp", g_t[:], bass.AP(g_d, 0, [[0, 128], [1, 128]]), [], [("c", "g%d" % i)], "c%d" % (1 + i))
        op("act", "mul", [("c", "g0")], [("c", "g0")], out=gq_bc[:], in_=gq_bc[:], mul=SCALE)
        op("dve", "memset", [], [("c", "onesb")], ap=onesb[:], constant=1.0)
        mqk, mqkk = newstat(2)
        op("dve", "tensor_reduce", [("c", "g0")], [mqkk[0]], out=mqk[:, 0:1], in_=gq_bc[:], axis=AX.X, op=ALU.max,
           apply_absolute_value=True)
        op("dve", "tensor_reduce", [("c", "g1")], [mqkk[1]], out=mqk[:, 1:2], in_=gk_bc[:], axis=AX.X, op=ALU.max,
           apply_absolute_value=True)
        op("dve", "tensor_scalar", mqkk, [("c", "negB")], out=negB[:, 0:1], in0=mqk[:, 0:1], scalar1=mqk[:, 1:2], scalar2=-128.0,
           op0=ALU.mult, op1=ALU.mult)

        dma("sp", stageT[0:16, 0:2, :], cond.ap().rearrange("r (c p) -> c r p", p=128), [], [("c", "stageT")], "c4")
        for r_ in range(2):
            op("pe", "transpose", [("c", "stageT"), ("c", "identf")], [pskey(0)], out=PS[:, r_ * 16:(r_ + 1) * 16],
               in_=stageT[0:16, r_, :], identity=identf[0:16, 0:16])
        op("dve", "tensor_copy", [pskey(0)], [("c", "condT")], out=condT[:, :, :],
           in_=PS[:, 0:32].rearrange("p (r c) -> p r c", r=2))
        op("act", "activation", [("c", "condT")], [("c", "scT")], out=scT[:, :, :].rearrange("p c r -> p r c"), in_=condT[:, :, :],
           func=AF.Silu)
        R4f = R4[:, :].bitcast(F32)
        GBXf = GBX[:, :].bitcast(F32)
        mloc = {0: ("R4", R4f[0:2, 0:2048], R4f[0:2, 2048:4096]), 1: ("GBX", GBXf[0:2, 0:2048], GBXf[0:2, 2048:4096])}
        mstate = {"n": 0}

        def mod_block(nb, b):
            v, q = divmod(nb, 4)
            reg, mv, gbuf = mloc[0 if v < 2 else 1]
            gsel = {1: 0, 2: 1, 4: 2, 5: 3}.get(v)
            if q == 0 and gsel is not None:
                dma("sp", gbuf, bass.AP(gvec[gsel], 0, [[0, 2], [1, D]]), [], [(reg, "gbuf")], "c6")
            mvq = mv[:, q * 512:(q + 1) * 512]
            dma("sp", mvq, bass.AP(b_mod, nb * 512, [[0, 2], [1, 512]]), [], [(reg, "mv", q)], "bm%d" % q)
            W, wk = w_get(mod_blocks[nb])
            for c in range(16):
                op("pe", "matmul", [wk, ("c", "scT")], [pskey(b)], out=bank(b)[0:2, :], lhsT=scT[:, c, :],
                   rhs=W[:, c, :], start=(c == 0), stop=(c == 15))
            w_done(mod_blocks[nb])
            op("dve", "tensor_tensor", [pskey(b), (reg, "mv", q)], [(reg, "mv", q)], out=mvq, in0=bank(b)[0:2, :], in1=mvq, op=ALU.add)
            if q == 3:
                mvk = [(reg, "mv", q_) for q_ in range(4)]
                if v in (1, 4):
                    op("dve", "scalar_tensor_tensor", mvk + [(reg, "gbuf")], mvk, out=mv, in0=mv, scalar=1.0,
                       in1=gbuf, op0=ALU.add, op1=ALU.mult)
                elif v in (2, 5):
                    op("dve", "tensor_tensor", mvk + [(reg, "gbuf")], mvk, out=mv, in0=mv, in1=gbuf, op=ALU.mult)
                dma("sp", modrows.ap()[v], mv, mvk, [("mod", v)], "mv%d" % v)

        for nb in range(8):
            mod_block(nb, 4 + nb % 4)

        def rstd_from_ss(ss_ap, ss_keys, n, inv_dim):
            sd, sdk = newstat(n)
            op("act", "activation", ss_keys + [("c", "eps")], sdk, out=sd, in_=ss_ap, func=AF.Sqrt,
               bias=epst[:, 0:1], scale=inv_dim)
            rs, rsk = newstat(n)
            op("dve", "reciprocal", sdk, rsk, out=rs, in_=sd)
            return rs, rsk

        def norm_transpose(g, t, src_ap, src_keys, acol, bcol, slot, nact=8):
            ss, ssk = newstat()
            op("act", "activation", src_keys, [("GBX", "xn", slot)] + ssk, out=xn[slot], in_=src_ap, func=AF.Square,
               accum_out=ss)
            rs, rsk = rstd_from_ss(ss, ssk, 1, 1.0 / D)
            op("dve", "tensor_scalar", src_keys + rsk, [("GBX", "xn", slot)], out=xn[slot], in0=src_ap,
               scalar1=rs, scalar2=None, op0=ALU.mult)
            b0 = 3 * slot

            def back():
                _norm_back(t, acol, bcol, slot, b0, nact)
            return back

        def _norm_back(t, acol, bcol, slot, b0, nact):
            def place(c):
                if nact == 8:
                    return b0 + c // 8, c % 8
                if c < 12:
                    return b0 + c // 6, c % 6
                return b0 + 2, c - 12
            for c in range(16):
                bb, cc = place(c)
                o = PSB[:, bb * 1024 + cc * 128: bb * 1024 + (cc + 1) * 128]
                op("pe", "transpose", [("GBX", "xn", slot), ("c", "identb")], [pskey(bb)], out=o,
                   in_=xn[slot][:, c * 128:(c + 1) * 128], identity=identb[:])
            for c in range(16):
                bb, cc = place(c)
                o = PSB[:, bb * 1024 + cc * 128: bb * 1024 + (cc + 1) * 128]
                if c < nact:
                    op("act", "activation", [pskey(bb), ("c", "modcols")], [("R2", "hT", t)],
                       out=hT[:, c, t * 128:(t + 1) * 128], in_=o, func=AF.Identity,
                       scale=modcols[:, acol, c:c + 1], bias=modcols[:, bcol, c:c + 1])
                else:
                    op("dve", "tensor_scalar", [pskey(bb), ("c", "modcols")], [("R2", "hT", t)],
                       out=hT[:, c, t * 128:(t + 1) * 128], in0=o, scalar1=modcols[:, acol, c:c + 1],
                       scalar2=modcols[:, bcol, c:c + 1], op0=ALU.mult, op1=ALU.add)

        qT = R4[:, 0:16384].rearrange("p (h t) -> p h t", h=16)
        kT = R4[:, 16384:16384 + 12800].rearrange("p (h t) -> p h t", h=10)
        Vt = R4[:, 29184:29184 + 12800].rearrange("p (c h d) -> p c h d", c=10, h=10)
        OT = hT
        RSf = RS[:, :].bitcast(F32)

        for g in range(2):
            x_d = xg[g]
            y_d = yg[g]
            def load_modcols(js):
                for j in js:
                    v = (1, 0, 4, 3)[j]
                    dma("sp", stageT[0:16, j, :], modrows.ap()[v, g:g + 1, :].rearrange("o (c p) -> (o c) p", p=128),
                        [("mod", v)], [("c", "stageT")], "c7")
                for j in js:
                    op("pe", "transpose", [("c", "stageT"), ("c", "identf")], [pskey(1)], out=PS[:, 512 + j * 16:512 + (j + 1) * 16],
                       in_=stageT[0:16, j, :], identity=identf[0:16, 0:16])
                j0_, j1_ = js[0], js[-1] + 1
                op("dve", "tensor_copy", [pskey(1)], [("c", "modcols")], out=modcols[:, j0_:j1_, :],
                   in_=PS[:, 512 + j0_ * 16:512 + j1_ * 16].rearrange("p (j c) -> p j c", c=16))
            load_modcols((0, 1) if g == 0 else (0, 1, 2, 3))

            pend_back = None
            for t in range(NT):
                sl = t % 2
                dma("sp", xt[sl], x_d[t * 128:(t + 1) * 128, :], [], [("RS", "x", sl)], "x%d" % sl)
                bk = norm_transpose(g, t, xt[sl], [("RS", "x", sl)], 0, 1, sl)
                if pend_back is not None:
                    pend_back()
                pend_back = bk
            pend_back()
            if g == 0:
                S.retire("R4")
                dma("sp", stageT[:, 0:3, :], conv_w.rearrange("j (c p) -> c j p", p=128), [], [("c", "stageT")], "c3")
                dma("sp", stageT[:, 3:4, :], conv_b.rearrange("j (c p) -> c j p", p=128), [], [("c", "stageT")], "c3")
                for j in range(4):
                    op("pe", "transpose", [("c", "stageT"), ("c", "identf")], [pskey(1)], out=PS[:, 512 + j * 88:512 + (j + 1) * 88],
                       in_=stageT[:, j, :], identity=identf[0:88, 0:88])
                op("dve", "tensor_copy", [pskey(1)], [("c", "taps")], out=taps[:, :, :],
                   in_=PS[:, 512:512 + 352].rearrange("p (j c) -> p j c", j=4))
                dma("sp", xt[0][0:64, 0:64], mask_d, [], [("RS", "x", 0)], "x0")
                dma("sp", xt[0][64:128, 0:64], mask_d, [], [("RS", "x", 0)], "x0")
                op("dve", "tensor_copy", [("RS", "x", 0)], [("c", "mask")], out=maskb[:], in_=xt[0][0:128, 0:64])
            S.retire("RS")
            S.retire("GBX")
            if stop_after == ("A", g):
                break

            if g == 1:
                ropeC = GB[:, 0:1024].rearrange("p (t d) -> p t d", t=8)
                ropeS = GB[:, 1024:2048].rearrange("p (t d) -> p t d", t=8)
                dma("sp", ropeC, ropeC_d.rearrange("(t p) d -> p t d", p=128), [], [("GBX", "ropeC")], "c8")
                dma("sp", ropeS, ropeS_d.rearrange("(t p) d -> p t d", p=128), [], [("GBX", "ropeS")], "c8b")
                for (src, nh, h0, isk) in ((cak, 8, 0, True), (cbk, 2, 8, True), (cav, 8, 0, False), (cbv, 2, 8, False)):
                    for ch in range(2):
                        for hh in range(0, nh, 4):
                            n = min(4, nh - hh)
                            stg = RSf[:, 0:n * 128].rearrange("p (h d) -> p h d", h=n)
                            dma("sp", stg, src[hh:hh + n, ch * 128:(ch + 1) * 128, :].rearrange("h s d -> s h d"),
                                [], [("RS", "stg", 0)], "cst")
                            if not isk:
                                op("dve", "tensor_copy", [("RS", "stg", 0)], [("R4", "V", ch)],
                                   out=Vt[:, ch, h0 + hh:h0 + hh + n, :], in_=stg)
                            else:
                                tb = RS[:, 6144:6144 + n * 128]
                                op("dve", "tensor_copy", [("RS", "stg", 0)], [("RS", "tmpb", 0)], out=tb,
                                   in_=RSf[:, 0:n * 128])
                                for i in range(n):
                                    op("pe", "transpose", [("RS", "tmpb", 0), ("c", "identb")], [pskey(7)],
                                       out=PSB[:, 7 * 1024 + i * 128:7 * 1024 + (i + 1) * 128],
                                       in_=tb[:, i * 128:(i + 1) * 128], identity=identb[:])
                                op("act", "copy", [pskey(7)], [("R4", "kTc")],
                                   out=kT[:, h0 + hh:h0 + hh + n, ch * 128:(ch + 1) * 128],
                                   in_=PSB[:, 7 * 1024:7 * 1024 + n * 128].rearrange("p (h t) -> p h t", h=n))
            in_blocks = WIDX[("in", g)]
            cnt = 0
            icnt = 0
            pendB = []
            for j in range(9):
                W, wk = w_get(in_blocks[j])
                for t in range(NT):
                    b = 4 + cnt % 4
                    cnt += 1
                    icnt += 1
                    for c in range(16):
                        op("pe", "matmul", [wk, ("R2", "hT", t)], [pskey(b)], out=bank(b), lhsT=hT[:, c, t * 128:(t + 1) * 128],
                           rhs=W[:, c, :], start=(c == 0), stop=(c == 15))
                    pb = bank(b)
                    sq_b = t // 2
                    s0 = (t % 2) * 128
                    st = icnt % 3
                    stg = RSf[:, st * 512:(st + 1) * 512]
                    stgk = ("RS", "stg", st)
                    tmpb = RS[:, 6144 + st * 512:6144 + (st + 1) * 512]
                    tmpk = ("RS", "tmpb", st)
                    if j in (0, 1):
                        op("act", "activation", [pskey(b)], [tmpk], out=tmpb, in_=pb, func=AF.Copy, scale=SCALE)
                        tr_src, nh, dst = tmpb, 4, qT[:, 4 * j:4 * j + 4, t * 128:(t + 1) * 128]
                        dstk = ("R4", "qT", t)
                    elif j in (2, 3):
                        h0 = 4 * (j - 2)
                        if g == 0:
                            op("act", "copy", [pskey(b)], [stgk], out=stg, in_=pb)
                            dma("sp", sak[sq_b, h0:h0 + 4, s0:s0 + 128, :].rearrange("h s d -> s h d"),
                                stg.rearrange("p (h d) -> p h d", h=4), [stgk], [], "stg%d" % st)
                            op("dve", "tensor_copy", [stgk], [tmpk], out=tmpb, in_=stg)
                        else:
                            op("act", "copy", [pskey(b)], [tmpk], out=tmpb, in_=pb)
                        tr_src, nh, dst = tmpb, 4, kT[:, h0:h0 + 4, 256 + t * 128:256 + (t + 1) * 128]
                        dstk = ("R4", "kT", t)
                    elif j in (4, 5):
                        h0 = 4 * (j - 4)
                        vdst = Vt[:, 2 + t, h0:h0 + 4, :]
                        if g == 0:
                            op("act", "copy", [pskey(b)], [stgk], out=stg, in_=pb)
                            dma("sp", sav[sq_b, h0:h0 + 4, s0:s0 + 128, :].rearrange("h s d -> s h d"),
                                stg.rearrange("p (h d) -> p h d", h=4), [stgk], [], "stg%d" % st)
                            op("dve", "tensor_copy", [stgk], [("R4", "V", 2 + t)], out=vdst,
                               in_=stg.rearrange("p (h d) -> p h d", h=4))
                        else:
                            op("act", "copy", [pskey(b)], [("R4", "V", 2 + t)], out=vdst,
                               in_=pb.rearrange("p (h d) -> p h d", h=4))
                        tr_src = None
                    else:
                        nh = 4 if j < 8 else 2
                        gbc = gq_bc if j < 8 else gk_bc
                        gkey = ("c", "g0") if j < 8 else ("c", "g1")
                        nw = nh * 128
                        op("act", "copy", [pskey(b)], [stgk], out=stg, in_=pb)
                        if j == 8:
                            vdst = Vt[:, 2 + t, 8:10, :]
                            if g == 0:
                                dma("sp", sbv[sq_b, :, s0:s0 + 128, :].rearrange("h s d -> s h d"),
                                    stg[:, 256:512].rearrange("p (h d) -> p h d", h=2), [stgk], [], "stgv%d" % st)
                            op("dve", "tensor_copy", [stgk], [("R4", "V", 2 + t)], out=vdst,
                               in_=stg[:, 256:512].rearrange("p (h d) -> p h d", h=2))
                        sqj = RSf[:, 1536 + st * 512:1536 + st * 512 + nw]
                        sqk = ("RS", "sqj", st)
                        ss, ssk = newstat(nh)
                        for hh in range(nh):
                            op("act", "activation", [stgk], [sqk, ssk[hh]], out=sqj[:, hh * 128:(hh + 1) * 128],
                               in_=stg[:, hh * 128:(hh + 1) * 128], func=AF.Square, accum_out=ss[:, hh:hh + 1])
                        rs, rsk = rstd_from_ss(ss, ssk, nh, 1.0 / 128)
                        s3 = stg[:, 0:nw].rearrange("p (h d) -> p h d", h=nh)
                        direct = (g == 0 and j < 8)
                        for hh in range(nh):
                            hs = slice(hh * 128, (hh + 1) * 128)
                            op("dve", "scalar_tensor_tensor", [stgk, gkey] + rsk, [tmpk if direct else stgk],
                               out=(tmpb[:, hs] if direct else stg[:, hs]), in0=stg[:, hs], scalar=rs[:, hh:hh + 1],
                               in1=gbc[:, :], op0=ALU.mult, op1=ALU.mult)
                        if g == 0:
                            if j == 8:
                                dma("sp", sbk[sq_b, :, s0:s0 + 128, :].rearrange("h s d -> s h d"), s3, [stgk], [],
                                    "stgk%d" % st)
                                op("pool", "tensor_copy", [stgk], [tmpk], out=tmpb[:, 0:nw], in_=stg[:, 0:nw])
                        else:
                            s5 = stg[:, 0:nw].rearrange("p (h a b d) -> p h a b d", h=nh, a=2, b=2)
                            q5 = sqj.rearrange("p (h a b d) -> p h a b d", h=nh, a=2, b=2)
                            S4 = ropeS[:, t, :].rearrange("p (a b d) -> p a b d", a=2, b=2)
                            for bb in range(2):
                                for aa in range(2):
                                    op("dve" if aa == 0 else "pool", "tensor_tensor", [stgk, ("GBX", "ropeS")], [sqk],
                                       out=q5[:, :, aa, bb, :], in0=s5[:, :, aa, 1 - bb, :],
                                       in1=S4[:, aa, bb, :].unsqueeze(1).to_broadcast([128, nh, 32]), op=ALU.mult)
                            op("pool", "tensor_tensor", [stgk, ("GBX", "ropeC")], [stgk], out=s3, in0=s3,
                               in1=ropeC[:, t, :].unsqueeze(1).to_broadcast([128, nh, 128]), op=ALU.mult)
                            op("dve", "tensor_tensor", [stgk, sqk], [tmpk], out=tmpb[:, 0:nw], in0=stg[:, 0:nw], in1=sqj,
                               op=ALU.add)
                        tr_src = tmpb
                        if j < 8:
                            hq = 8 + 4 * (j - 6)
                            dst = qT[:, hq:hq + 4, t * 128:(t + 1) * 128]
                            dstk = ("R4", "qT", t)
                        else:
                            dst = kT[:, 8:10, 256 + t * 128:256 + (t + 1) * 128]
                            dstk = ("R4", "kT", t)
                    if tr_src is not None:
                        def _tr(tr_src=tr_src, nh=nh, dst=dst, dstk=dstk, tmpk=tmpk, cnt=icnt, j=j):
                            tbk = 0 + cnt % 4
                            for i in range(nh):
                                op("pe", "transpose", [tmpk, ("c", "identb")], [pskey(tbk)],
                                   out=PSB[:, tbk * 1024 + i * 128:tbk * 1024 + (i + 1) * 128],
                                   in_=tr_src[:, i * 128:(i + 1) * 128], identity=identb[:])
                            src3 = PSB[:, tbk * 1024:tbk * 1024 + nh * 128].rearrange("p (h t) -> p h t", h=nh)
                            if g == 1 and j in (0, 1):
                                for rr in range(2):
                                    op("dve" if cnt % 2 else "act", "tensor_copy" if cnt % 2 else "copy", [pskey(tbk)], [dstk],
                                       out=dst[:, :, rr * 64:(rr + 1) * 64], in_=src3[:, :, rr * 64:(rr + 1) * 64][:, :, ::-1])
                            else:
                                op("dve" if cnt % 2 else "act", "tensor_copy" if cnt % 2 else "copy", [pskey(tbk)], [dstk], out=dst,
                                   in_=src3)
                        pendB.append(_tr)
                    while len(pendB) > 2:
                        pendB.pop(0)()
                    if g == 0 and t == 3:
                        mod_block(8 + j, 4 + cnt % 4)
                        cnt += 1
                if j in (3, 8):
                    while pendB:
                        pendB.pop(0)()
                w_done(in_blocks[j])
            S.retire("RS")
            S.retire("GBX")
            S.retire("R2")
            if stop_after == ("B", g):
                break

            S.retire("R4")
            S.retire("ps")
            if g == 1:
                Tq = GBX[:, 0:3840].rearrange("p (h r k) -> p h r k", h=4, r=15)
                for h_ in range(8):
                    p0_ = (h_ % 2) * 64
                    dma("pool", GBX[p0_:p0_ + 64, (h_ // 2) * 960:(h_ // 2 + 1) * 960].rearrange("p (a k) -> p a k", k=64),
                        bass.AP(rpbpad, 64 - 48 + h_ * 15 * 31, [[1, 64], [31, 15], [1, 64]]), [], [("GBX", "Tq")], "c9")
                Tq3 = GBX[:, 0:3840].rearrange("p (a k) -> p a k", k=64)
                op("dve", "tensor_tensor", [("GBX", "Tq"), ("c", "mask")], [("GBX", "Tq")], out=Tq3, in0=Tq3,
                   in1=maskb[:, :].unsqueeze(1).to_broadcast([128, 60, 64]), op=ALU.add)

            PTc = [RS[:, i * 512:(i + 1) * 512] for i in range(4)]
            rzb = [RSf[:, 1024 + i * 512:1024 + (i + 1) * 512] for i in range(2)]
            bun = []
            if g == 0:
                for sq in range(4):
                    for h in range(8):
                        bun.append(dict(h=h, q0=sq * 256, n=256, k0=256 + sq * 256, nch=2, v0=2 + 2 * sq))
            else:
                for h in range(8):
                    for qb in range(2):
                        bun.append(dict(h=h, q0=qb * 512, n=512, k0=0, nch=10, v0=0))
            steps = [(ui, c) for ui, u in enumerate(bun) for c in range(u["nch"])]
            LAG = 3

            stb = {}
            stc = [0]

            def bX(i):
                ui, c = steps[i]
                u = bun[ui]
                kh = 8 + u["h"] // 4
                if g == 0 and i % 8 == 4 and 17 + i // 8 < 24:
                    mod_block(17 + i // 8, stc[0] % 4)
                    stc[0] += 1
                b = stc[0] % 4
                stc[0] += 1
                stb[i] = b
                op("pe", "matmul", [("R4", "qT", 0), ("R4", "kT", 0)], [pskey(b)], out=bank(b)[:, 0:u["n"]],
                   lhsT=kT[:, kh, u["k0"] + c * 128:u["k0"] + (c + 1) * 128], rhs=qT[:, 8 + u["h"], u["q0"]:u["q0"] + u["n"]],
                   start=True, stop=True)
                pb_ = i % 4
                op("act", "activation", [pskey(b), ("c", "negB")], [("RS", "PTc", pb_)], out=PTc[pb_][:, 0:u["n"]],
                   in_=bank(b)[:, 0:u["n"]], func=AF.Exp, bias=negB[:, 0:1], scale=1.0)

            def bZ(i):
                ui, c = steps[i]
                u = bun[ui]
                kh = 8 + u["h"] // 4
                b = i % 4
                n = u["n"]
                ob, zb = 4 + ui % 2, 6 + ui % 2
                op("pe", "matmul", [("RS", "PTc", b), ("R4", "V", 0)], [pskey(ob)], out=bank(ob)[:, 0:n],
                   lhsT=Vt[:, u["v0"] + c, kh, :], rhs=PTc[b][:, 0:n], start=(c == 0), stop=(c == u["nch"] - 1))
                op("pe", "matmul", [("RS", "PTc", b), ("c", "onesb")], [pskey(zb)], out=bank(zb)[:, 0:n],
                   lhsT=onesb[:, :], rhs=PTc[b][:, 0:n], start=(c == 0), stop=(c == u["nch"] - 1))
                if c == u["nch"] - 1:
                    rz = rzb[ui % 2][:, 0:n]
                    if g == 0:
                        op("act", "activation", [pskey(zb)], [("RS", "rz", ui % 2)], out=rz, in_=bank(zb)[:, 0:n], func=AF.Ln)
                        op("act", "activation", [("RS", "rz", ui % 2)], [("RS", "rz", ui % 2)], out=rz, in_=rz, func=AF.Exp,
                           scale=-1.0)
                    else:
                        op("dve", "reciprocal", [pskey(zb)], [("RS", "rz", ui % 2)], out=rz, in_=bank(zb)[:, 0:n])
                    op("dve", "tensor_tensor", [pskey(ob), ("RS", "rz", ui % 2)],
                       [("R2", "OT", t_) for t_ in range(u["q0"] // 128, (u["q0"] + n) // 128)],
                       out=OT[:, 8 + u["h"], u["q0"]:u["q0"] + n], in0=bank(ob)[:, 0:n], in1=rz, op=ALU.mult)

            for i in range(len(steps) + LAG):
                if i < len(steps):
                    bX(i)
                if i >= LAG:
                    bZ(i - LAG)
            S.retire("ps")
            S.retire("RS")

            NPB = 4
            Pb = [RS[:, s * 896:(s + 1) * 896] for s in range(NPB)]
            PTs = [RS[:, 3584 + s * 896:3584 + (s + 1) * 896] for s in range(2)]
            Ssbs = [RSf[:, 2688:2688 + 896]]
            units = []
            if g == 0:
                for sq in range(4):
                    for h in range(8):
                        for qb in range(2):
                            tok = sq * 256 + qb * 128
                            units.append(dict(nq=128, pair=False,
                                              hds=[dict(p0=0, np=128, q=qT[:, h, tok:tok + 128],
                                                        ks=[kT[:, h, 256 + sq * 256:256 + sq * 256 + 256]],
                                                        vs=[Vt[:, 2 + 2 * sq + i, h, :] for i in range(2)], bias2=None)],
                                              out=OT[:, h, tok:tok + 128], bias=None, pad=False, okey=("R2", "OT", tok // 128)))
            else:
                for r in range(16):
                    rs_ = min(max(r - 4, 0), 8)
                    odd = rs_ % 2 == 1
                    c0 = (rs_ - 1) // 2 if odd else rs_ // 2
                    for hp in range(4):
                        dr0 = rs_ - r + 7
                        hds = []
                        for i_ in range(2):
                            h = 2 * hp + i_
                            hds.append(dict(p0=64 * i_, np=64, q=qT[:, h, r * 64:(r + 1) * 64],
                                            ks=[kT[:, h, 256 + rs_ * 64:256 + rs_ * 64 + 512], kT[:, h, 0:256]],
                                            vs=[Vt[:, 2 + c0 + i, h, :] for i in range(5 if odd else 4)] + [Vt[:, i, h, :] for i in range(2)],
                                            bias2=GBX[64 * i_:64 * i_ + 64, hp * 960 + dr0 * 64:hp * 960 + (dr0 + 8) * 64]))
                        units.append(dict(nq=128, pair=True, hds=hds,
                                          out=OT[:, 2 * hp:2 * hp + 2, r * 64:(r + 1) * 64],
                                          bias=Tq[:, hp, dr0:dr0 + 8, :], pad=odd, okey=("R2", "OT", r // 2)))

            def stage1(u, ui):
                sb0 = 2 * (ui % 3)
                u["sb0"] = sb0
                for hd in u["hds"]:
                    p0, np_ = hd["p0"], hd["np"]
                    col = 0
                    for kseg in hd["ks"]:
                        n = kseg.shape[-1]
                        o = 0
                        while o < n:
                            w = min(512, n - o)
                            bsel = sb0 + col // 512
                            fuse = u["bias"] is not None and not u["pad"] and col == 0
                            kw = dict(tile_position=(0, p0)) if p0 else {}
                            op("pe", "matmul", [("R4", "qT", 0), ("R4", "kT", 0)], [pskey(bsel)],
                               out=PS[p0:p0 + np_, sb0 * 512 + col:sb0 * 512 + col + w], lhsT=hd["q"], rhs=kseg[:, o:o + w],
                               start=True, stop=not fuse, **kw)
                            if fuse:
                                kw2 = dict(tile_position=(p0, p0)) if p0 else {}
                                op("pe", "matmul", [("GBX", "Tq"), ("c", "identb")], [pskey(bsel)],
                                   out=PS[p0:p0 + np_, sb0 * 512:sb0 * 512 + 512], lhsT=identb[p0:p0 + 64, p0:p0 + 64],
                                   rhs=hd["bias2"], start=False, stop=True, **kw2)
                            o += w
                            col += w
                    u["lk"] = col

            def stage2a(u, ui):
                nq, lk, sb0 = u["nq"], u["lk"], u["sb0"]
                nb = (lk + 511) // 512
                sk = [pskey(sb0 + i) for i in range(nb)]
                src = PS[0:nq, sb0 * 512:sb0 * 512 + lk]
                skeys = sk
                if u["bias"] is not None and u["pad"]:
                    Ssb = Ssbs[0]
                    ssk_ = ("RS", "Ssb", 0)
                    off = 64 if u["pad"] else 0
                    tot = lk + (128 if u["pad"] else 0)
                    if u["pad"]:
                        op("pool", "memset", [], [ssk_], ap=Ssb[0:nq, 0:64], constant=NEG)
                        op("pool", "memset", [], [ssk_], ap=Ssb[0:nq, 576:640], constant=NEG)
                    op("dve", "tensor_tensor", [sk[0], ("GBX", "Tq")], [ssk_],
                       out=Ssb[0:nq, off:off + 512].rearrange("p (r k) -> p r k", r=8),
                       in0=PS[0:nq, sb0 * 512:sb0 * 512 + 512].rearrange("p (r k) -> p r k", r=8), in1=u["bias"], op=ALU.add)
                    op("act", "copy", [sk[1]], [ssk_], out=Ssb[0:nq, tot - 256:tot],
                       in_=PS[0:nq, sb0 * 512 + 512:sb0 * 512 + 768])
                    src = Ssb[0:nq, 0:tot]
                    skeys = [ssk_]
                    lk = tot
                    u["lk"] = tot
                pi = ui % NPB
                P = Pb[pi][0:nq, 0:lk]
                pk = ("RS", "P", pi)
                nmx, nmk = newstat()
                op("dve", "tensor_reduce", skeys, nmk, out=nmx[0:nq], in_=src, axis=AX.X, op=ALU.max, negate=True)
                rsum, rsk = newstat()
                op("act", "activation", skeys + nmk, [pk] + rsk, out=P, in_=src, func=AF.Exp, bias=nmx[0:nq], scale=1.0,
                   accum_out=rsum[0:nq])
                rinv, rik = newstat()
                op("dve", "reciprocal", rsk, rik, out=rinv[0:nq], in_=rsum[0:nq])
                op("dve", "tensor_scalar", [pk] + rik, [pk], out=P, in0=P, scalar1=rinv[0:nq], scalar2=None, op0=ALU.mult)

            def stage2b(u, ui):
                nq, lk = u["nq"], u["lk"]
                pi = ui % NPB
                pk = ("RS", "P", pi)
                nch = lk // 128
                ps_ = 0 if nch * nq > 512 else ui % 2
                ptb = 6 * 1024 + ps_ * 512
                ptk = ("ps", "PT", ps_)
                for c in range(nch):
                    op("pe", "transpose", [pk, ("c", "identb")], [ptk], out=PSB[:, ptb + c * nq:ptb + (c + 1) * nq],
                       in_=Pb[pi][0:nq, c * 128:(c + 1) * 128], identity=identb[0:nq, 0:nq])
                ptsk = ("RS", "PT", ui % 2)
                PT = PTs[ui % 2][:, 0:nch * nq].rearrange("p (c q) -> p c q", c=nch)
                op("act", "copy", [ptk], [ptsk], out=PT,
                   in_=PSB[:, ptb:ptb + nch * nq].rearrange("p (c q) -> p c q", c=nch))
                ovs = ui % 4
                OV = PS[:, 7 * 512 + ovs * 128:7 * 512 + ovs * 128 + nq]
                for hd in u["hds"]:
                    p0, np_ = hd["p0"], hd["np"]
                    for c in range(nch):
                        op("pe", "matmul", [ptsk, ("R4", "V", 0)], [("ps", "OV", ovs)], out=OV[:, p0:p0 + np_], lhsT=hd["vs"][c],
                           rhs=PT[:, c, p0:p0 + np_], start=(c == 0), stop=(c == nch - 1))
                if u["pair"]:
                    op("act", "copy", [("ps", "OV", ovs)], [u["okey"]], out=u["out"],
                       in_=OV.rearrange("p (i q) -> p i q", i=2)[:, :, ::-1])
                else:
                    op("act", "copy", [("ps", "OV", ovs)], [u["okey"]], out=u["out"], in_=OV)

            nu = len(units)
            for ui in range(nu + 3):
                if ui < nu:
                    stage1(units[ui], ui)
                if 0 <= ui - 1 < nu:
                    stage2a(units[ui - 1], ui - 1)
                if 0 <= ui - 3:
                    stage2b(units[ui - 3], ui - 3)
            S.retire("ps")
            S.retire("R4")
            S.retire("RS")
            S.retire("GBX")
            if stop_after == ("C", g):
                break

            if g == 0:
                load_modcols((2, 3))
            o_sb = R4[:, 0:32768].bitcast(F32).rearrange("p (t d) -> p t d", t=8)
            dma("sp", GB, bass.AP(modrows, (2 * 2 + g) * D, [[0, 128], [1, D]]), [("mod", 2)], [("GBX", "G")], "c10")
            out_blocks = WIDX[("out", g)]
            ssq = {}
            junk = GBX[:, 4096:4096 + 512]
            for j in range(4):
                W, wk = w_get(out_blocks[j])
                for t in range(NT):
                    b = cnt % 8
                    cnt += 1
                    for c in range(16):
                        op("pe", "matmul", [wk, ("R2", "OT", t)], [pskey(b)], out=bank(b), lhsT=OT[:, c, t * 128:(t + 1) * 128],
                           rhs=W[:, c, :], start=(c == 0), stop=(c == 15))
                    if j == 0:
                        ssq[t] = newstat(4)
                    ss4, ss4k = ssq[t]
                    op("dve", "tensor_tensor", [pskey(b), ("GBX", "G")], [("R4", "o", t)], out=o_sb[:, t, j * 512:(j + 1) * 512],
                       in0=bank(b), in1=GB[:, j * 512:(j + 1) * 512], op=ALU.mult)
                    op("act", "activation", [("R4", "o", t), pskey(b)], [("GBX", "junk"), ss4k[j]], out=junk,
                       in_=bank(b), func=AF.Square, accum_out=ss4[:, j:j + 1])
                w_done(out_blocks[j])
            S.retire("R2")
            if stop_after == ("D1", g):
                break
            def dpost_stages(t):
                sl = t % 2
                st_ = {}

                def s0():
                    dma("sp", xt[sl], x_d[t * 128:(t + 1) * 128, :], [], [("RS", "x", sl)], "x%d" % sl)

                def s1():
                    ss4, ss4k = ssq[t]
                    st_["ss"] = newstat()
                    op("dve", "tensor_reduce", ss4k, st_["ss"][1], out=st_["ss"][0], in_=ss4, axis=AX.X, op=ALU.add)

                def s2():
                    st_["sd"] = newstat()
                    op("act", "activation", st_["ss"][1] + [("c", "eps")], st_["sd"][1], out=st_["sd"][0], in_=st_["ss"][0],
                       func=AF.Sqrt, bias=epst[:, 0:1], scale=1.0 / D)

                def s3():
                    st_["rs"] = newstat()
                    op("dve", "reciprocal", st_["sd"][1], st_["rs"][1], out=st_["rs"][0], in_=st_["sd"][0])

                def s4():
                    op("dve", "scalar_tensor_tensor", [("R4", "o", t), ("RS", "x", sl)] + st_["rs"][1], [("R4", "o", t)],
                       out=o_sb[:, t, :], in0=o_sb[:, t, :], scalar=st_["rs"][0], in1=xt[sl], op0=ALU.mult, op1=ALU.add)

                def s5():
                    dma("sp", y_d[t * 128:(t + 1) * 128, :], o_sb[:, t, :], [("R4", "o", t)], [("y", g, t)], "ys%d" % t)
                    st_["s2"] = newstat()
                    op("act", "activation", [("R4", "o", t)], [("GBX", "xn", sl)] + st_["s2"][1], out=xn[sl], in_=o_sb[:, t, :],
                       func=AF.Square, accum_out=st_["s2"][0])

                def s6():
                    st_["d2"] = newstat()
                    op("act", "activation", st_["s2"][1] + [("c", "eps")], st_["d2"][1], out=st_["d2"][0], in_=st_["s2"][0],
                       func=AF.Sqrt, bias=epst[:, 0:1], scale=1.0 / D)

                def s7():
                    st_["r2"] = newstat()
                    op("dve", "reciprocal", st_["d2"][1], st_["r2"][1], out=st_["r2"][0], in_=st_["d2"][0])

                def s8():
                    op("dve", "tensor_scalar", [("R4", "o", t)] + st_["r2"][1], [("GBX", "xn", sl)], out=xn[sl], in0=o_sb[:, t, :],
                       scalar1=st_["r2"][0], scalar2=None, op0=ALU.mult)

                def back():
                    _norm_back(t, 2, 3, sl, 3 * sl, 12)
                return [s0, s1, s2, s3, s4, s5, s6, s7, s8], back

            prev_backs = []
            for t0_ in range(0, NT, 2):
                pair = [dpost_stages(t0_), dpost_stages(t0_ + 1)]
                for si in range(9):
                    for stg_, _ in pair:
                        stg_[si]()
                    if si == 4:
                        for bk in prev_backs:
                            bk()
                        prev_backs = []
                prev_backs = [bk for _, bk in pair]
            for bk in prev_backs:
                bk()
            S.retire("R4")
            S.retire("RS")
            S.retire("GBX")
            if stop_after == ("D", g):
                break

            f_sb = R4[:, 0:32768].bitcast(F32).rearrange("p (t d) -> p t d", t=8)
            gT = R4[:, 32768:32768 + 11264].rearrange("p (k t) -> p k t", k=11)
            dma("sp", GB, bass.AP(modrows, (5 * 2 + g) * D, [[0, 128], [1, D]]), [("mod", 5)], [("GBX", "G")], "c10")
            nseq = 4 if g == 0 else 1
            sl_len = 1024 // nseq
            for qd in range(4):
                for (c0, ncz) in EGROUPS:
                    i0 = 11 * qd + c0
                    bv = WIDX[("up", g, qd, c0)]
                    Wvg, wvk = w_get(bv)
                    wgk = wvk
                    Wv = Wvg[:, :, 0:256]
                    Wg = Wvg[:, :, 256:512]
                    for ci in range(ncz):
                        i = i0 + ci
                        il = c0 + ci
                        pb0 = 4 * (cnt % 2)
                        cnt += 1
                        ek = cnt % 2
                        for c in range(16):
                            for half, (Wx, wxk) in enumerate(((Wv, wvk), (Wg, wgk))):
                                for tb in range(2):
                                    b = pb0 + half * 2 + tb
                                    op("pe", "matmul", [wxk] + [("R2", "hT", tb * 4 + k) for k in range(4)], [pskey(b)],
                                       out=bank(b), lhsT=Wx[:, c, ci * 128:(ci + 1) * 128], rhs=hT[:, c, tb * 512:(tb + 1) * 512],
                                       start=(c == 0), stop=(c == 15))
                        acc = []
                        for half in range(2):
                            ch = i if half == 0 else 44 + i
                            a = RSf[:, ek * 2048 + half * 1024:ek * 2048 + (half + 1) * 1024]
                            ak = ("RS", "acc", ek, half)
                            acc.append((a, ak))
                            for tb in range(2):
                                b = pb0 + half * 2 + tb
                                op("act", "activation", [pskey(b), ("c", "taps")], [ak], out=a[:, tb * 512:(tb + 1) * 512],
                                   in_=bank(b), func=AF.Identity, scale=taps[:, 1, ch:ch + 1], bias=taps[:, 3, ch:ch + 1])
                            for tb in range(2):
                                b = pb0 + half * 2 + tb
                                nsb = 512 // sl_len if sl_len < 512 else 1
                                ln = min(512, sl_len)
                                a3 = a[:, tb * 512:(tb + 1) * 512].rearrange("p (s l) -> p s l", s=nsb)
                                p3 = bank(b).rearrange("p (s l) -> p s l", s=nsb)
                                op("dve", "scalar_tensor_tensor", [pskey(b), ("c", "taps"), ak], [ak], out=a3[:, :, 1:ln],
                                   in0=p3[:, :, 0:ln - 1], scalar=taps[:, 0, ch:ch + 1], in1=a3[:, :, 1:ln], op0=ALU.mult, op1=ALU.add)
                                op("dve", "scalar_tensor_tensor", [pskey(b), ("c", "taps"), ak], [ak], out=a3[:, :, 0:ln - 1],
                                   in0=p3[:, :, 1:ln], scalar=taps[:, 2, ch:ch + 1], in1=a3[:, :, 0:ln - 1], op0=ALU.mult, op1=ALU.add)
                            if sl_len > 512:
                                b0_, b1_ = pb0 + half * 2, pb0 + half * 2 + 1
                                op("dve", "scalar_tensor_tensor", [pskey(b0_), ("c", "taps"), ak], [ak], out=a[:, 512:513],
                                   in0=bank(b0_)[:, 511:512], scalar=taps[:, 0, ch:ch + 1], in1=a[:, 512:513], op0=ALU.mult, op1=ALU.add)
                                op("dve", "scalar_tensor_tensor", [pskey(b1_), ("c", "taps"), ak], [ak], out=a[:, 511:512],
                                   in0=bank(b1_)[:, 0:1], scalar=taps[:, 2, ch:ch + 1], in1=a[:, 511:512], op0=ALU.mult, op1=ALU.add)
                        (av, avk), (ag, agk) = acc
                        sg = GBX[:, 4096 + ek * 2048:4096 + (ek + 1) * 2048].bitcast(F32) if False else None
                        sgt = xn[ek].bitcast(F32) if False else None
                        op("act", "activation", [agk], [agk], out=ag, in_=ag, func=AF.Silu)
                        op("dve", "tensor_tensor", [agk, avk], [("R4", "gT", il)], out=gT[:, il, :], in0=ag, in1=av, op=ALU.mult)
                    w_done(bv)
                for j in range(4):
                    bd = WIDX[("dn", g, qd)][j]
                    Wd, wdk = w_get(bd)
                    fbanks = {}
                    if j == 0:
                        for t in range(NT):
                            fbanks[t] = cnt % 8
                            cnt += 1
                            for k in range(10):
                                op("pe", "matmul", [wdk, ("R4", "gT", k)], [pskey(fbanks[t])], out=bank(fbanks[t]),
                                   lhsT=gT[:, k, t * 128:(t + 1) * 128], rhs=Wd[:, k, :], start=(k == 0), stop=False)
                    for t in range(NT):
                        if j == 0:
                            b = fbanks[t]
                            krange = range(10, 11)
                        else:
                            b = cnt % 8
                            cnt += 1
                            krange = range(11)
                        for k in krange:
                            op("pe", "matmul", [wdk, ("R4", "gT", k)], [pskey(b)], out=bank(b), lhsT=gT[:, k, t * 128:(t + 1) * 128],
                               rhs=Wd[:, k, :], start=(k == 0), stop=(k == 10))
                        fo = f_sb[:, t, j * 512:(j + 1) * 512]
                        if qd == 0:
                            op("act", "copy", [pskey(b)], [("R4", "f", t)], out=fo, in_=bank(b))
                        else:
                            op("dve", "tensor_tensor", [pskey(b), ("R4", "f", t)], [("R4", "f", t)], out=fo, in0=bank(b), in1=fo,
                               op=ALU.add)
                    w_done(bd)
            S.retire("RS")
            S.retire("R2")
            def final_stages(t):
                sl = t % 2
                st_ = {}

                def s0():
                    dma("sp", xt[sl], y_d[t * 128:(t + 1) * 128, :], [("y", g, t)], [("RS", "x", sl)], "x%d" % sl)

                def s1():
                    st_["ss"] = newstat()
                    op("act", "activation", [("R4", "f", t)], [("GBX", "xn", sl)] + st_["ss"][1], out=xn[sl], in_=f_sb[:, t, :],
                       func=AF.Square, accum_out=st_["ss"][0])

                def s2():
                    st_["sd"] = newstat()
                    op("act", "activation", st_["ss"][1] + [("c", "eps")], st_["sd"][1], out=st_["sd"][0], in_=st_["ss"][0],
                       func=AF.Sqrt, bias=epst[:, 0:1], scale=1.0 / D)

                def s3():
                    st_["rs"] = newstat()
                    op("dve", "reciprocal", st_["sd"][1], st_["rs"][1], out=st_["rs"][0], in_=st_["sd"][0])

                def s4():
                    op("dve", "scalar_tensor_tensor", [("R4", "f", t), ("GBX", "G")] + st_["rs"][1], [("R4", "f", t)],
                       out=f_sb[:, t, :], in0=f_sb[:, t, :], scalar=st_["rs"][0], in1=GB, op0=ALU.mult, op1=ALU.mult)

                def s5():
                    op("dve", "tensor_tensor", [("R4", "f", t), ("RS", "x", sl)], [("R4", "f", t)], out=f_sb[:, t, :],
                       in0=f_sb[:, t, :], in1=xt[sl], op=ALU.add)

                def s6():
                    dma("sp", y_d[t * 128:(t + 1) * 128, :], f_sb[:, t, :], [("R4", "f", t)], [("y", g, t)], "xo%d" % sl)
                return [s0, s1, s2, s3, s4, s5, s6]

            fpairs = [[final_stages(t0_), final_stages(t0_ + 1)] for t0_ in range(0, NT, 2)]
            for pi_, pair in enumerate(fpairs):
                for si in range(7):
                    if si == 0 and pi_ > 0:
                        continue
                    for stg_ in pair:
                        stg_[si]()
                    if si == 5 and pi_ + 1 < len(fpairs):
                        for stg_ in fpairs[pi_ + 1]:
                            stg_[0]()
            S.retire("R4")
            S.retire("RS")
            S.retire("GBX")

        sems = {e: es.enter_context(nc.semaphore("s_" + e)) for e in COMPUTE}
        dsems = {k: es.enter_context(nc.semaphore("d_" + k)) for k in dsem_keys}
        with nc.Block() as block:
            S.emit_block(block, sems, dsems)
    return nc


def _consts():
    half = 64
    freqs = (10000.0 ** (-np.arange(0, half, 2, dtype=np.float32) / half)).astype(np.float32)
    t = np.arange(T)
    ar = (t // 64).astype(np.float32)[:, None] * freqs[None, :]
    ac = (t % 64).astype(np.float32)[:, None] * freqs[None, :]
    C = np.concatenate([np.cos(ar), np.cos(ar), np.cos(ac), np.cos(ac)], axis=1).astype(np.float32)
    Sn = np.concatenate([-np.sin(ar), np.sin(ar), -np.sin(ac), np.sin(ac)], axis=1).astype(np.float32)
    col = np.arange(64)
    cs = np.clip(col - 8, 0, 48)
    valid = (col[None, :] >= cs[:, None]) & (col[None, :] < cs[:, None] + 16)
    mask = np.where(valid, 0.0, NEG).astype(np.float32)[::-1].copy()
    return C, Sn, mask, np.eye(128, dtype=np.float32)


_PROG = {}


def kernel(x_prompt, x_sample, c, cache_a_k, cache_a_v, cache_b_k, cache_b_v, c_ctx, w_mod, b_mod,
           g_attn_pre, g_attn_post, g_ffn_pre, g_ffn_post, w_in, rpb, g_qnorm, g_knorm, w_out, w_up,
           conv_w, conv_b, w_down, _stop_after=None):
    f = lambda a: np.ascontiguousarray(np.asarray(a, dtype=np.float32))
    key = _stop_after
    if key not in _PROG:
        _PROG[key] = build_program(_stop_after)
    nc = _PROG[key]
    C, Sn, mask, ident = _consts()
    x_prompt, x_sample, c = f(x_prompt), f(x_sample), f(c)
    shared = {
        "w_mod": f(w_mod)[0], "b_mod": f(b_mod), "g_attn_pre": f(g_attn_pre), "g_attn_post": f(g_attn_post),
        "g_ffn_pre": f(g_ffn_pre), "g_ffn_post": f(g_ffn_post), "w_in": f(w_in)[0],
        "rpbpad": np.pad(f(rpb).reshape(-1), 64)[None, :].copy(), "g_qnorm": f(g_qnorm), "g_knorm": f(g_knorm),
        "w_out": f(w_out)[0], "w_up": f(w_up)[0], "conv_w": f(conv_w)[0], "conv_b": f(conv_b), "w_down": f(w_down)[0],
        "ident": ident, "ropeC": C, "ropeS": Sn, "maskrev": mask,
    }
    in_maps = []
    for i in range(8):
        m = dict(shared)
        m["xp"] = x_prompt[4 * i:4 * i + 4].reshape(T, D)
        m["xs"] = x_sample[i]
        m["cond"] = np.stack([f(c_ctx), c[i]], axis=0)
        m["cak"] = f(cache_a_k)[i, 0]
        m["cav"] = f(cache_a_v)[i, 0]
        m["cbk"] = f(cache_b_k)[i, 0]
        m["cbv"] = f(cache_b_v)[i, 0]
        in_maps.append(m)
    res = run_bass_kernel_spmd(nc, in_maps, core_ids=list(range(8)))
    R = res.results
    yp = np.concatenate([R[i]["yp"].reshape(4, 256, D) for i in range(8)], axis=0)
    ys = np.stack([R[i]["ys"] for i in range(8)], axis=0)
    outs = [yp, ys]
    for n in ("sak", "sav", "sbk", "sbv"):
        outs.append(np.concatenate([R[i][n] for i in range(8)], axis=0)[:, None])
    return tuple(np.ascontiguousarray(o, dtype=np.float32) for o in outs)
```

```python
import numpy as np
from contextlib import ExitStack
import concourse.bass as bass
import concourse.mybir as mybir
from concourse.bass_utils import run_bass_kernel_spmd

F32 = mybir.dt.float32
BF16 = mybir.dt.bfloat16
AF = mybir.ActivationFunctionType
ALU = mybir.AluOpType
AX = mybir.AxisListType

D = 2048
T = 1024
NT = 8
DFF = 5632
EPS = 1e-6
SCALE = 128 ** -0.5
NEG = -30000.0
COMPUTE = ("pe", "act", "dve", "pool")


class Op:
    __slots__ = ("eng", "fn", "deps", "signal", "sem", "val", "dma", "pos")

    def __init__(self, eng, fn, dma):
        self.eng = eng
        self.fn = fn
        self.deps = []
        self.signal = False
        self.sem = None
        self.val = 0
        self.dma = dma


class Res:
    __slots__ = ("w", "readers")

    def __init__(self):
        self.w = None
        self.readers = {}


class Sched:
    def __init__(self):
        self.streams = {e: [] for e in ("pe", "act", "dve", "pool", "sp")}
        self.res = {}
        self.fence = {}
        self.dma_cnt = {}

    def _get(self, key):
        r = self.res.get(key)
        if r is None:
            r = Res()
            self.res[key] = r
            f = self.fence.get(key[0])
            if f:
                r.w = f
        return r

    @staticmethod
    def _pos(o):
        return o.val if o.dma else o.pos

    def retire(self, region):
        ops = list(self.fence.get(region) or [])
        for k in [k for k in self.res if k[0] == region]:
            r = self.res.pop(k)
            if r.w is not None:
                ops.extend(r.w if isinstance(r.w, list) else [r.w])
            ops.extend(r.readers.values())
        best = {}
        for o in ops:
            ch = ("d", o.sem) if o.dma else ("c", o.eng)
            b = best.get(ch)
            if b is None or self._pos(o) > self._pos(b):
                best[ch] = o
        self.fence[region] = list(best.values())

    def add(self, eng, fn, reads=(), writes=(), dma_sem=None):
        op = Op(eng, fn, dma_sem is not None)
        if op.dma:
            c = self.dma_cnt.get(dma_sem, 0) + 16
            self.dma_cnt[dma_sem] = c
            op.sem = dma_sem
            op.val = c
        deps = {}

        def need(d, war):
            if d is op:
                return
            if not d.dma and not op.dma and d.eng == eng and (eng == "pe" or (war and eng != "pool")):
                return
            if op.dma and d.dma and d.sem == op.sem and not war:
                return
            deps[id(d)] = d

        for k in reads:
            r = self._get(k)
            if r.w is not None:
                for d in (r.w if isinstance(r.w, list) else [r.w]):
                    need(d, False)
        for k in writes:
            r = self._get(k)
            if r.w is not None:
                for d in (r.w if isinstance(r.w, list) else [r.w]):
                    need(d, False)
            for d in r.readers.values():
                need(d, True)
        ch = ("d", dma_sem) if op.dma else eng
        for k in reads:
            self.res[k].readers[ch] = op
        for k in writes:
            r = self.res[k]
            r.w = op
            r.readers = {}
        op.deps = list(deps.values())
        for d in op.deps:
            d.signal = True
        st = self.streams[eng]
        op.pos = len(st)
        st.append(op)
        return op

    def emit_block(self, block, sems, dma_sems):
        for eng in COMPUTE:
            c = 0
            for op in self.streams[eng]:
                if not op.dma:
                    if op.signal:
                        c += 1
                        op.val = c
                    op.sem = eng
        S = self

        def run(eng, e):
            known = {}
            for op in S.streams[eng]:
                need = {}
                for d in op.deps:
                    key = ("d", d.sem) if d.dma else ("c", d.sem)
                    if need.get(key, (0, None))[0] < d.val:
                        need[key] = (d.val, d)
                for key, (v, d) in need.items():
                    if known.get(key, 0) < v:
                        e.wait_ge(dma_sems[d.sem] if d.dma else sems[d.sem], v)
                        known[key] = v
                ins = op.fn(e)
                if op.dma:
                    ins.then_inc(dma_sems[op.sem], 16)
                elif op.signal:
                    ins.then_inc(sems[eng], 1)

        @block.tensor
        def _(e):
            run("pe", e)

        @block.scalar
        def _(e):
            run("act", e)

        @block.vector
        def _(e):
            run("dve", e)

        @block.gpsimd
        def _(e):
            run("pool", e)

        @block.sync
        def _(e):
            run("sp", e)
            for k, c in S.dma_cnt.items():
                e.wait_ge(dma_sems[k], c)


def build_program(stop_after=None):
    nc = bass.Bass("TRN2", target_bir_lowering=False)

    def din(name, shape):
        return nc.dram_tensor(name, list(shape), F32, kind="ExternalInput")

    def dout(name, shape):
        return nc.dram_tensor(name, list(shape), F32, kind="ExternalOutput")

    xg = [din("xp", (T, D)).ap(), din("xs", (T, D)).ap()]
    cond = din("cond", (2, D))
    cak = din("cak", (8, 256, 128)).ap()
    cav = din("cav", (8, 256, 128)).ap()
    cbk = din("cbk", (2, 256, 128)).ap()
    cbv = din("cbv", (2, 256, 128)).ap()
    w_mod = din("w_mod", (D, 6 * D)).ap()
    b_mod = din("b_mod", (1, 6 * D))
    gvec = [din(n, (1, D)) for n in ("g_attn_pre", "g_attn_post", "g_ffn_pre", "g_ffn_post")]
    w_in = din("w_in", (D, 4608)).ap()
    rpbpad = din("rpbpad", (1, 120 * 31 + 128))
    g_qn = din("g_qnorm", (1, 128))
    g_kn = din("g_knorm", (1, 128))
    w_out = din("w_out", (D, D)).ap()
    w_up = din("w_up", (D, 2 * DFF)).ap()
    conv_w = din("conv_w", (3, 2 * DFF)).ap()
    conv_b = din("conv_b", (1, 2 * DFF)).ap()
    w_down = din("w_down", (DFF, D)).ap()
    ident_d = din("ident", (128, 128)).ap()
    ropeC_d = din("ropeC", (T, 128)).ap()
    ropeS_d = din("ropeS", (T, 128)).ap()
    mask_d = din("maskrev", (64, 64)).ap()

    yg = [dout("yp", (T, D)).ap(), dout("ys", (T, D)).ap()]
    sak = dout("sak", (4, 8, 256, 128)).ap()
    sav = dout("sav", (4, 8, 256, 128)).ap()
    sbk = dout("sbk", (4, 2, 256, 128)).ap()
    sbv = dout("sbv", (4, 2, 256, 128)).ap()
    modrows = nc.dram_tensor("modrows", [6, 2, D], F32, kind="Internal")

    S = Sched()
    dsem_keys = []

    def dk(k):
        if k not in dsem_keys:
            dsem_keys.append(k)
        return k

    def op(eng, method, reads, writes, **kw):
        return S.add(eng, lambda e, m=method, kw=kw: getattr(e, m)(**kw), reads, writes)

    def dma(eng, out, in_, reads, writes, sem):
        return S.add(eng, lambda e, o=out, i=in_: e.dma_start(out=o, in_=i), reads, writes, dma_sem=dk(sem))

    with ExitStack() as es:
        es.enter_context(nc.allow_non_contiguous_dma(reason="small strided param loads"))

        def sb(name, shape, dt):
            return es.enter_context(nc.sbuf_tensor(name, list(shape), dt))

        R2 = sb("R2", (128, 16384), BF16)
        R3 = sb("R3", (128, 3 * 8192), BF16)
        R4 = sb("R4", (128, 44032), BF16)
        RS = sb("RS", (128, 8192), BF16)
        GBX = sb("GBX", (128, 8192), BF16)
        identf = sb("identf", (128, 128), F32)
        identb = sb("identb", (128, 128), BF16)
        modcols = sb("modcols", (128, 4, 16), F32)
        taps = sb("taps", (128, 4, 88), F32)
        stats = sb("stats", (128, 256), F32)
        epst = sb("epst", (128, 1), F32)
        gq_bc = sb("gq_bc", (128, 128), F32)
        gk_bc = sb("gk_bc", (128, 128), F32)
        maskb = sb("maskb", (128, 64), BF16)
        onesb = sb("onesb", (128, 128), BF16)
        negB = sb("negB", (128, 2), F32)
        condT = sb("condT", (128, 2, 16), F32)
        scT = sb("scT", (128, 16, 2), BF16)
        stageT = sb("stageT", (88, 4, 128), F32)
        PS = es.enter_context(nc.psum_tensor("PS", [128, 4096], F32))
        PSB = PS[:, :].bitcast(BF16)

        def bank(b, n=512):
            return PS[:, b * 512:b * 512 + n]

        def pskey(b):
            return ("ps", b)

        hT = R2[:, :].rearrange("p (c t) -> p c t", c=16)
        slots = [R3[:, s * 8192:(s + 1) * 8192] for s in range(3)]
        xt = [RS[:, s * 4096:(s + 1) * 4096].bitcast(F32) for s in range(2)]
        GB = GBX[:, 0:4096].bitcast(F32)
        xn = [GBX[:, 4096 + s * 2048:4096 + (s + 1) * 2048] for s in range(2)]

        stat_i = [0]

        def newstat(n=1):
            i = stat_i[0]
            if i + n > 256:
                i = 0
            stat_i[0] = i + n
            return stats[:, i:i + n], [("st", j) for j in range(i, i + n)]

        wq = []

        def wq_add(dram_ap, shape, parts=None):
            wq.append((dram_ap, shape, parts))
            return len(wq) - 1

        wstate = {"issued": 0}

        def w_issue(n):
            while wstate["issued"] <= n and wstate["issued"] < len(wq):
                i = wstate["issued"]
                src, shape, parts = wq[i]
                s = i % 3
                a, b = shape
                dst = slots[s][:, 0:a * b].rearrange("p (a b) -> p a b", a=a)
                if parts is None:
                    dma("pool", dst, src, [], [("w", s)], "w%d" % s)
                else:
                    for (psrc, c0_, c1_) in parts:
                        dma("pool", dst[:, :, c0_:c1_], psrc, [], [("w", s)], "w%d" % s)
                wstate["issued"] += 1

        def w_get(n):
            assert n >= wstate.get("last", 0), ("weight blocks must be consumed in registration order", n, wstate.get("last"))
            wstate["last"] = n
            assert n < 3 or (n - 3) in wdone, ("slot still in use", n)
            w_issue(n)
            a, b = wq[n][1]
            return slots[n % 3][:, 0:a * b].rearrange("p (a b) -> p a b", a=a), ("w", n % 3)

        wdone = set()

        def w_done(n):
            wdone.add(n)
            while wstate["issued"] < len(wq) and (wstate["issued"] < 3 or (wstate["issued"] - 3) in wdone):
                w_issue(wstate["issued"])

        w_mod3 = w_mod.rearrange("(c p) n -> p c n", p=128)
        w_in3 = w_in.rearrange("(c p) n -> p c n", p=128)
        w_out3 = w_out.rearrange("(c p) n -> p c n", p=128)
        w_up3 = w_up.rearrange("(c p) n -> p c n", p=128)
        w_dn3 = w_down.rearrange("(k p) n -> p k n", p=128)
        EGROUPS = [(0, 2), (2, 2), (4, 2), (6, 2), (8, 2), (10, 1)]
        mod_blocks = {}
        for nb in range(8):
            mod_blocks[nb] = wq_add(w_mod3[:, :, nb * 512:(nb + 1) * 512], (16, 512))
        WIDX = {}
        for g_ in range(2):
            WIDX[("in", g_)] = []
            for j in range(9):
                WIDX[("in", g_)].append(wq_add(w_in3[:, :, j * 512:(j + 1) * 512], (16, 512)))
                if g_ == 0:
                    nb = 8 + j
                    mod_blocks[nb] = wq_add(w_mod3[:, :, nb * 512:(nb + 1) * 512], (16, 512))
            if g_ == 0:
                for nb in range(17, 24):
                    mod_blocks[nb] = wq_add(w_mod3[:, :, nb * 512:(nb + 1) * 512], (16, 512))
            WIDX[("out", g_)] = [wq_add(w_out3[:, :, j * 512:(j + 1) * 512], (16, 512)) for j in range(4)]
            for qd_ in range(4):
                for (c0_, ncz_) in EGROUPS:
                    i0_ = 11 * qd_ + c0_
                    WIDX[("up", g_, qd_, c0_)] = wq_add(None, (16, 512), parts=[
                        (w_up3[:, :, i0_ * 128:(i0_ + ncz_) * 128], 0, ncz_ * 128),
                        (w_up3[:, :, DFF + i0_ * 128:DFF + (i0_ + ncz_) * 128], 256, 256 + ncz_ * 128)])
                WIDX[("dn", g_, qd_)] = [wq_add(w_dn3[:, 11 * qd_:11 * qd_ + 11, j * 512:(j + 1) * 512], (11, 512)) for j in range(4)]

        dma("sp", identf[:], ident_d, [], [("c", "identf")], "c0")
        op("dve", "tensor_copy", [("c", "identf")], [("c", "identb")], out=identb[:], in_=identf[:])
        op("dve", "memset", [], [("c", "eps")], ap=epst[:], constant=EPS)
        for i, (g_d, g_t) in enumerate(((g_qn, gq_bc), (g_kn, gk_bc))):
            dma("sp", g_t[:], bass.AP(g_d, 0, [[0, 128], [1, 128]]), [], [("c", "g%d" % i)], "c%d" % (1 + i))
        op("act", "mul", [("c", "g0")], [("c", "g0")], out=gq_bc[:], in_=gq_bc[:], mul=SCALE)
        op("dve", "memset", [], [("c", "onesb")], ap=onesb[:], constant=1.0)
        mqk, mqkk = newstat(2)
        op("dve", "tensor_reduce", [("c", "g0")], [mqkk[0]], out=mqk[:, 0:1], in_=gq_bc[:], axis=AX.X, op=ALU.max,
           apply_absolute_value=True)
        op("dve", "tensor_reduce", [("c", "g1")], [mqkk[1]], out=mqk[:, 1:2], in_=gk_bc[:], axis=AX.X, op=ALU.max,
           apply_absolute_value=True)
        op("dve", "tensor_scalar", mqkk, [("c", "negB")], out=negB[:, 0:1], in0=mqk[:, 0:1], scalar1=mqk[:, 1:2], scalar2=-128.0,
           op0=ALU.mult, op1=ALU.mult)

        dma("sp", stageT[0:16, 0:2, :], cond.ap().rearrange("r (c p) -> c r p", p=128), [], [("c", "stageT")], "c4")
        for r_ in range(2):
            op("pe", "transpose", [("c", "stageT"), ("c", "identf")], [pskey(0)], out=PS[:, r_ * 16:(r_ + 1) * 16],
               in_=stageT[0:16, r_, :], identity=identf[0:16, 0:16])
        op("dve", "tensor_copy", [pskey(0)], [("c", "condT")], out=condT[:, :, :],
           in_=PS[:, 0:32].rearrange("p (r c) -> p r c", r=2))
        op("act", "activation", [("c", "condT")], [("c", "scT")], out=scT[:, :, :].rearrange("p c r -> p r c"), in_=condT[:, :, :],
           func=AF.Silu)
        R4f = R4[:, :].bitcast(F32)
        GBXf = GBX[:, :].bitcast(F32)
        mloc = {0: ("R4", R4f[0:2, 0:2048], R4f[0:2, 2048:4096]), 1: ("GBX", GBXf[0:2, 0:2048], GBXf[0:2, 2048:4096])}
        mstate = {"n": 0}

        def mod_block(nb, b):
            v, q = divmod(nb, 4)
            reg, mv, gbuf = mloc[0 if v < 2 else 1]
            gsel = {1: 0, 2: 1, 4: 2, 5: 3}.get(v)
            if q == 0 and gsel is not None:
                dma("sp", gbuf, bass.AP(gvec[gsel], 0, [[0, 2], [1, D]]), [], [(reg, "gbuf")], "c6")
            mvq = mv[:, q * 512:(q + 1) * 512]
            dma("sp", mvq, bass.AP(b_mod, nb * 512, [[0, 2], [1, 512]]), [], [(reg, "mv", q)], "bm%d" % q)
            W, wk = w_get(mod_blocks[nb])
            for c in range(16):
                op("pe", "matmul", [wk, ("c", "scT")], [pskey(b)], out=bank(b)[0:2, :], lhsT=scT[:, c, :],
                   rhs=W[:, c, :], start=(c == 0), stop=(c == 15))
            w_done(mod_blocks[nb])
            op("dve", "tensor_tensor", [pskey(b), (reg, "mv", q)], [(reg, "mv", q)], out=mvq, in0=bank(b)[0:2, :], in1=mvq, op=ALU.add)
            if q == 3:
                mvk = [(reg, "mv", q_) for q_ in range(4)]
                if v in (1, 4):
                    op("dve", "scalar_tensor_tensor", mvk + [(reg, "gbuf")], mvk, out=mv, in0=mv, scalar=1.0,
                       in1=gbuf, op0=ALU.add, op1=ALU.mult)
                elif v in (2, 5):
                    op("dve", "tensor_tensor", mvk + [(reg, "gbuf")], mvk, out=mv, in0=mv, in1=gbuf, op=ALU.mult)
                dma("sp", modrows.ap()[v], mv, mvk, [("mod", v)], "mv%d" % v)

        for nb in range(8):
            mod_block(nb, 4 + nb % 4)

        def rstd_from_ss(ss_ap, ss_keys, n, inv_dim):
            sd, sdk = newstat(n)
            op("act", "activation", ss_keys + [("c", "eps")], sdk, out=sd, in_=ss_ap, func=AF.Sqrt,
               bias=epst[:, 0:1], scale=inv_dim)
            rs, rsk = newstat(n)
            op("dve", "reciprocal", sdk, rsk, out=rs, in_=sd)
            return rs, rsk

        def norm_transpose(g, t, src_ap, src_keys, acol, bcol, slot, nact=8):
            ss, ssk = newstat()
            op("act", "activation", src_keys, [("GBX", "xn", slot)] + ssk, out=xn[slot], in_=src_ap, func=AF.Square,
               accum_out=ss)
            rs, rsk = rstd_from_ss(ss, ssk, 1, 1.0 / D)
            op("dve", "tensor_scalar", src_keys + rsk, [("GBX", "xn", slot)], out=xn[slot], in0=src_ap,
               scalar1=rs, scalar2=None, op0=ALU.mult)
            b0 = 3 * slot

            def back():
                _norm_back(t, acol, bcol, slot, b0, nact)
            return back

        def _norm_back(t, acol, bcol, slot, b0, nact):
            def place(c):
                if nact == 8:
                    return b0 + c // 8, c % 8
                if c < 12:
                    return b0 + c // 6, c % 6
                return b0 + 2, c - 12
            for c in range(16):
                bb, cc = place(c)
                o = PSB[:, bb * 1024 + cc * 128: bb * 1024 + (cc + 1) * 128]
                op("pe", "transpose", [("GBX", "xn", slot), ("c", "identb")], [pskey(bb)], out=o,
                   in_=xn[slot][:, c * 128:(c + 1) * 128], identity=identb[:])
            for c in range(16):
                bb, cc = place(c)
                o = PSB[:, bb * 1024 + cc * 128: bb * 1024 + (cc + 1) * 128]
                if c < nact:
                    op("act", "activation", [pskey(bb), ("c", "modcols")], [("R2", "hT", t)],
                       out=hT[:, c, t * 128:(t + 1) * 128], in_=o, func=AF.Identity,
                       scale=modcols[:, acol, c:c + 1], bias=modcols[:, bcol, c:c + 1])
                else:
                    op("dve", "tensor_scalar", [pskey(bb), ("c", "modcols")], [("R2", "hT", t)],
                       out=hT[:, c, t * 128:(t + 1) * 128], in0=o, scalar1=modcols[:, acol, c:c + 1],
                       scalar2=modcols[:, bcol, c:c + 1], op0=ALU.mult, op1=ALU.add)

        qT = R4[:, 0:16384].rearrange("p (h t) -> p h t", h=16)
        kT = R4[:, 16384:16384 + 12800].rearrange("p (h t) -> p h t", h=10)
        Vt = R4[:, 29184:29184 + 12800].rearrange("p (c h d) -> p c h d", c=10, h=10)
        OT = hT
        RSf = RS[:, :].bitcast(F32)

        for g in range(2):
            x_d = xg[g]
            y_d = yg[g]
            def load_modcols(js):
                for j in js:
                    v = (1, 0, 4, 3)[j]
                    dma("sp", stageT[0:16, j, :], modrows.ap()[v, g:g + 1, :].rearrange("o (c p) -> (o c) p", p=128),
                        [("mod", v)], [("c", "stageT")], "c7")
                for j in js:
                    op("pe", "transpose", [("c", "stageT"), ("c", "identf")], [pskey(1)], out=PS[:, 512 + j * 16:512 + (j + 1) * 16],
                       in_=stageT[0:16, j, :], identity=identf[0:16, 0:16])
                j0_, j1_ = js[0], js[-1] + 1
                op("dve", "tensor_copy", [pskey(1)], [("c", "modcols")], out=modcols[:, j0_:j1_, :],
                   in_=PS[:, 512 + j0_ * 16:512 + j1_ * 16].rearrange("p (j c) -> p j c", c=16))
            load_modcols((0, 1) if g == 0 else (0, 1, 2, 3))

            pend_back = None
            for t in range(NT):
                sl = t % 2
                dma("sp", xt[sl], x_d[t * 128:(t + 1) * 128, :], [], [("RS", "x", sl)], "x%d" % sl)
                bk = norm_transpose(g, t, xt[sl], [("RS", "x", sl)], 0, 1, sl)
                if pend_back is not None:
                    pend_back()
                pend_back = bk
            pend_back()
            if g == 0:
                S.retire("R4")
                dma("sp", stageT[:, 0:3, :], conv_w.rearrange("j (c p) -> c j p", p=128), [], [("c", "stageT")], "c3")
                dma("sp", stageT[:, 3:4, :], conv_b.rearrange("j (c p) -> c j p", p=128), [], [("c", "stageT")], "c3")
                for j in range(4):
                    op("pe", "transpose", [("c", "stageT"), ("c", "identf")], [pskey(1)], out=PS[:, 512 + j * 88:512 + (j + 1) * 88],
                       in_=stageT[:, j, :], identity=identf[0:88, 0:88])
                op("dve", "tensor_copy", [pskey(1)], [("c", "taps")], out=taps[:, :, :],
                   in_=PS[:, 512:512 + 352].rearrange("p (j c) -> p j c", j=4))
                dma("sp", xt[0][0:64, 0:64], mask_d, [], [("RS", "x", 0)], "x0")
                dma("sp", xt[0][64:128, 0:64], mask_d, [], [("RS", "x", 0)], "x0")
                op("dve", "tensor_copy", [("RS", "x", 0)], [("c", "mask")], out=maskb[:], in_=xt[0][0:128, 0:64])
            S.retire("RS")
            S.retire("GBX")
            if stop_after == ("A", g):
                break

            if g == 1:
                ropeC = GB[:, 0:1024].rearrange("p (t d) -> p t d", t=8)
                ropeS = GB[:, 1024:2048].rearrange("p (t d) -> p t d", t=8)
                dma("sp", ropeC, ropeC_d.rearrange("(t p) d -> p t d", p=128), [], [("GBX", "ropeC")], "c8")
                dma("sp", ropeS, ropeS_d.rearrange("(t p) d -> p t d", p=128), [], [("GBX", "ropeS")], "c8b")
                for (src, nh, h0, isk) in ((cak, 8, 0, True), (cbk, 2, 8, True), (cav, 8, 0, False), (cbv, 2, 8, False)):
                    for ch in range(2):
                        for hh in range(0, nh, 4):
                            n = min(4, nh - hh)
                            stg = RSf[:, 0:n * 128].rearrange("p (h d) -> p h d", h=n)
                            dma("sp", stg, src[hh:hh + n, ch * 128:(ch + 1) * 128, :].rearrange("h s d -> s h d"),
                                [], [("RS", "stg", 0)], "cst")
                            if not isk:
                                op("dve", "tensor_copy", [("RS", "stg", 0)], [("R4", "V", ch)],
                                   out=Vt[:, ch, h0 + hh:h0 + hh + n, :], in_=stg)
                            else:
                                tb = RS[:, 6144:6144 + n * 128]
                                op("dve", "tensor_copy", [("RS", "stg", 0)], [("RS", "tmpb", 0)], out=tb,
                                   in_=RSf[:, 0:n * 128])
                                for i in range(n):
                                    op("pe", "transpose", [("RS", "tmpb", 0), ("c", "identb")], [pskey(7)],
                                       out=PSB[:, 7 * 1024 + i * 128:7 * 1024 + (i + 1) * 128],
                                       in_=tb[:, i * 128:(i + 1) * 128], identity=identb[:])
                                op("act", "copy", [pskey(7)], [("R4", "kTc")],
                                   out=kT[:, h0 + hh:h0 + hh + n, ch * 128:(ch + 1) * 128],
                                   in_=PSB[:, 7 * 1024:7 * 1024 + n * 128].rearrange("p (h t) -> p h t", h=n))
            in_blocks = WIDX[("in", g)]
            cnt = 0
            icnt = 0
            pendB = []
            for j in range(9):
                W, wk = w_get(in_blocks[j])
                for t in range(NT):
                    b = 4 + cnt % 4
                    cnt += 1
                    icnt += 1
                    for c in range(16):
                        op("pe", "matmul", [wk, ("R2", "hT", t)], [pskey(b)], out=bank(b), lhsT=hT[:, c, t * 128:(t + 1) * 128],
                           rhs=W[:, c, :], start=(c == 0), stop=(c == 15))
                    pb = bank(b)
                    sq_b = t // 2
                    s0 = (t % 2) * 128
                    st = icnt % 3
                    stg = RSf[:, st * 512:(st + 1) * 512]
                    stgk = ("RS", "stg", st)
                    tmpb = RS[:, 6144 + st * 512:6144 + (st + 1) * 512]
                    tmpk = ("RS", "tmpb", st)
                    if j in (0, 1):
                        op("act", "activation", [pskey(b)], [tmpk], out=tmpb, in_=pb, func=AF.Copy, scale=SCALE)
                        tr_src, nh, dst = tmpb, 4, qT[:, 4 * j:4 * j + 4, t * 128:(t + 1) * 128]
                        dstk = ("R4", "qT", t)
                    elif j in (2, 3):
                        h0 = 4 * (j - 2)
                        if g == 0:
                            op("act", "copy", [pskey(b)], [stgk], out=stg, in_=pb)
                            dma("sp", sak[sq_b, h0:h0 + 4, s0:s0 + 128, :].rearrange("h s d -> s h d"),
                                stg.rearrange("p (h d) -> p h d", h=4), [stgk], [], "stg%d" % st)
                            op("dve", "tensor_copy", [stgk], [tmpk], out=tmpb, in_=stg)
                        else:
                            op("act", "copy", [pskey(b)], [tmpk], out=tmpb, in_=pb)
                        tr_src, nh, dst = tmpb, 4, kT[:, h0:h0 + 4, 256 + t * 128:256 + (t + 1) * 128]
                        dstk = ("R4", "kT", t)
                    elif j in (4, 5):
                        h0 = 4 * (j - 4)
                        vdst = Vt[:, 2 + t, h0:h0 + 4, :]
                        if g == 0:
                            op("act", "copy", [pskey(b)], [stgk], out=stg, in_=pb)
                            dma("sp", sav[sq_b, h0:h0 + 4, s0:s0 + 128, :].rearrange("h s d -> s h d"),
                                stg.rearrange("p (h d) -> p h d", h=4), [stgk], [], "stg%d" % st)
                            op("dve", "tensor_copy", [stgk], [("R4", "V", 2 + t)], out=vdst,
                               in_=stg.rearrange("p (h d) -> p h d", h=4))
                        else:
                            op("act", "copy", [pskey(b)], [("R4", "V", 2 + t)], out=vdst,
                               in_=pb.rearrange("p (h d) -> p h d", h=4))
                        tr_src = None
                    else:
                        nh = 4 if j < 8 else 2
                        gbc = gq_bc if j < 8 else gk_bc
                        gkey = ("c", "g0") if j < 8 else ("c", "g1")
                        nw = nh * 128
                        op("act", "copy", [pskey(b)], [stgk], out=stg, in_=pb)
                        if j == 8:
                            vdst = Vt[:, 2 + t, 8:10, :]
                            if g == 0:
                                dma("sp", sbv[sq_b, :, s0:s0 + 128, :].rearrange("h s d -> s h d"),
                                    stg[:, 256:512].rearrange("p (h d) -> p h d", h=2), [stgk], [], "stgv%d" % st)
                            op("dve", "tensor_copy", [stgk], [("R4", "V", 2 + t)], out=vdst,
                               in_=stg[:, 256:512].rearrange("p (h d) -> p h d", h=2))
                        sqj = RSf[:, 1536 + st * 512:1536 + st * 512 + nw]
                        sqk = ("RS", "sqj", st)
                        ss, ssk = newstat(nh)
                        for hh in range(nh):
                            op("act", "activation", [stgk], [sqk, ssk[hh]], out=sqj[:, hh * 128:(hh + 1) * 128],
                               in_=stg[:, hh * 128:(hh + 1) * 128], func=AF.Square, accum_out=ss[:, hh:hh + 1])
                        rs, rsk = rstd_from_ss(ss, ssk, nh, 1.0 / 128)
                        s3 = stg[:, 0:nw].rearrange("p (h d) -> p h d", h=nh)
                        direct = (g == 0 and j < 8)
                        for hh in range(nh):
                            hs = slice(hh * 128, (hh + 1) * 128)
                            op("dve", "scalar_tensor_tensor", [stgk, gkey] + rsk, [tmpk if direct else stgk],
                               out=(tmpb[:, hs] if direct else stg[:, hs]), in0=stg[:, hs], scalar=rs[:, hh:hh + 1],
                               in1=gbc[:, :], op0=ALU.mult, op1=ALU.mult)
                        if g == 0:
                            if j == 8:
                                dma("sp", sbk[sq_b, :, s0:s0 + 128, :].rearrange("h s d -> s h d"), s3, [stgk], [],
                                    "stgk%d" % st)
                                op("pool", "tensor_copy", [stgk], [tmpk], out=tmpb[:, 0:nw], in_=stg[:, 0:nw])
                        else:
                            s5 = stg[:, 0:nw].rearrange("p (h a b d) -> p h a b d", h=nh, a=2, b=2)
                            q5 = sqj.rearrange("p (h a b d) -> p h a b d", h=nh, a=2, b=2)
                            S4 = ropeS[:, t, :].rearrange("p (a b d) -> p a b d", a=2, b=2)
                            for bb in range(2):
                                for aa in range(2):
                                    op("dve" if aa == 0 else "pool", "tensor_tensor", [stgk, ("GBX", "ropeS")], [sqk],
                                       out=q5[:, :, aa, bb, :], in0=s5[:, :, aa, 1 - bb, :],
                                       in1=S4[:, aa, bb, :].unsqueeze(1).to_broadcast([128, nh, 32]), op=ALU.mult)
                            op("pool", "tensor_tensor", [stgk, ("GBX", "ropeC")], [stgk], out=s3, in0=s3,
                               in1=ropeC[:, t, :].unsqueeze(1).to_broadcast([128, nh, 128]), op=ALU.mult)
                            op("dve", "tensor_tensor", [stgk, sqk], [tmpk], out=tmpb[:, 0:nw], in0=stg[:, 0:nw], in1=sqj,
                               op=ALU.add)
                        tr_src = tmpb
                        if j < 8:
                            hq = 8 + 4 * (j - 6)
                            dst = qT[:, hq:hq + 4, t * 128:(t + 1) * 128]
                            dstk = ("R4", "qT", t)
                        else:
                            dst = kT[:, 8:10, 256 + t * 128:256 + (t + 1) * 128]
                            dstk = ("R4", "kT", t)
                    if tr_src is not None:
                        def _tr(tr_src=tr_src, nh=nh, dst=dst, dstk=dstk, tmpk=tmpk, cnt=icnt, j=j):
                            tbk = 0 + cnt % 4
                            for i in range(nh):
                                op("pe", "transpose", [tmpk, ("c", "identb")], [pskey(tbk)],
                                   out=PSB[:, tbk * 1024 + i * 128:tbk * 1024 + (i + 1) * 128],
                                   in_=tr_src[:, i * 128:(i + 1) * 128], identity=identb[:])
                            src3 = PSB[:, tbk * 1024:tbk * 1024 + nh * 128].rearrange("p (h t) -> p h t", h=nh)
                            if g == 1 and j in (0, 1):
                                for rr in range(2):
                                    op("dve" if cnt % 2 else "act", "tensor_copy" if cnt % 2 else "copy", [pskey(tbk)], [dstk],
                                       out=dst[:, :, rr * 64:(rr + 1) * 64], in_=src3[:, :, rr * 64:(rr + 1) * 64][:, :, ::-1])
                            else:
                                op("dve" if cnt % 2 else "act", "tensor_copy" if cnt % 2 else "copy", [pskey(tbk)], [dstk], out=dst,
                                   in_=src3)
                        pendB.append(_tr)
                    while len(pendB) > 2:
                        pendB.pop(0)()
                    if g == 0 and t == 3:
                        mod_block(8 + j, 4 + cnt % 4)
                        cnt += 1
                if j in (3, 8):
                    while pendB:
                        pendB.pop(0)()
                w_done(in_blocks[j])
            S.retire("RS")
            S.retire("GBX")
            S.retire("R2")
            if stop_after == ("B", g):
                break

            S.retire("R4")
            S.retire("ps")
            if g == 1:
                Tq = GBX[:, 0:3840].rearrange("p (h r k) -> p h r k", h=4, r=15)
                for h_ in range(8):
                    p0_ = (h_ % 2) * 64
                    dma("pool", GBX[p0_:p0_ + 64, (h_ // 2) * 960:(h_ // 2 + 1) * 960].rearrange("p (a k) -> p a k", k=64),
                        bass.AP(rpbpad, 64 - 48 + h_ * 15 * 31, [[1, 64], [31, 15], [1, 64]]), [], [("GBX", "Tq")], "c9")
                Tq3 = GBX[:, 0:3840].rearrange("p (a k) -> p a k", k=64)
                op("dve", "tensor_tensor", [("GBX", "Tq"), ("c", "mask")], [("GBX", "Tq")], out=Tq3, in0=Tq3,
                   in1=maskb[:, :].unsqueeze(1).to_broadcast([128, 60, 64]), op=ALU.add)

            PTc = [RS[:, i * 512:(i + 1) * 512] for i in range(4)]
            rzb = [RSf[:, 1024 + i * 512:1024 + (i + 1) * 512] for i in range(2)]
            bun = []
            if g == 0:
                for sq in range(4):
                    for h in range(8):
                        bun.append(dict(h=h, q0=sq * 256, n=256, k0=256 + sq * 256, nch=2, v0=2 + 2 * sq))
            else:
                for h in range(8):
                    for qb in range(2):
                        bun.append(dict(h=h, q0=qb * 512, n=512, k0=0, nch=10, v0=0))
            steps = [(ui, c) for ui, u in enumerate(bun) for c in range(u["nch"])]
            LAG = 3

            stb = {}
            stc = [0]

            def bX(i):
                ui, c = steps[i]
                u = bun[ui]
                kh = 8 + u["h"] // 4
                if g == 0 and i % 8 == 4 and 17 + i // 8 < 24:
                    mod_block(17 + i // 8, stc[0] % 4)
                    stc[0] += 1
                b = stc[0] % 4
                stc[0] += 1
                stb[i] = b
                op("pe", "matmul", [("R4", "qT", 0), ("R4", "kT", 0)], [pskey(b)], out=bank(b)[:, 0:u["n"]],
                   lhsT=kT[:, kh, u["k0"] + c * 128:u["k0"] + (c + 1) * 128], rhs=qT[:, 8 + u["h"], u["q0"]:u["q0"] + u["n"]],
                   start=True, stop=True)
                pb_ = i % 4
                op("act", "activation", [pskey(b), ("c", "negB")], [("RS", "PTc", pb_)], out=PTc[pb_][:, 0:u["n"]],
                   in_=bank(b)[:, 0:u["n"]], func=AF.Exp, bias=negB[:, 0:1], scale=1.0)

            def bZ(i):
                ui, c = steps[i]
                u = bun[ui]
                kh = 8 + u["h"] // 4
                b = i % 4
                n = u["n"]
                ob, zb = 4 + ui % 2, 6 + ui % 2
                op("pe", "matmul", [("RS", "PTc", b), ("R4", "V", 0)], [pskey(ob)], out=bank(ob)[:, 0:n],
                   lhsT=Vt[:, u["v0"] + c, kh, :], rhs=PTc[b][:, 0:n], start=(c == 0), stop=(c == u["nch"] - 1))
                op("pe", "matmul", [("RS", "PTc", b), ("c", "onesb")], [pskey(zb)], out=bank(zb)[:, 0:n],
                   lhsT=onesb[:, :], rhs=PTc[b][:, 0:n], start=(c == 0), stop=(c == u["nch"] - 1))
                if c == u["nch"] - 1:
                    rz = rzb[ui % 2][:, 0:n]
                    if g == 0:
                        op("act", "activation", [pskey(zb)], [("RS", "rz", ui % 2)], out=rz, in_=bank(zb)[:, 0:n], func=AF.Ln)
                        op("act", "activation", [("RS", "rz", ui % 2)], [("RS", "rz", ui % 2)], out=rz, in_=rz, func=AF.Exp,
                           scale=-1.0)
                    else:
                        op("dve", "reciprocal", [pskey(zb)], [("RS", "rz", ui % 2)], out=rz, in_=bank(zb)[:, 0:n])
                    op("dve", "tensor_tensor", [pskey(ob), ("RS", "rz", ui % 2)],
                       [("R2", "OT", t_) for t_ in range(u["q0"] // 128, (u["q0"] + n) // 128)],
                       out=OT[:, 8 + u["h"], u["q0"]:u["q0"] + n], in0=bank(ob)[:, 0:n], in1=rz, op=ALU.mult)

            for i in range(len(steps) + LAG):
                if i < len(steps):
                    bX(i)
                if i >= LAG:
                    bZ(i - LAG)
            S.retire("ps")
            S.retire("RS")

            NPB = 4
            Pb = [RS[:, s * 896:(s + 1) * 896] for s in range(NPB)]
            PTs = [RS[:, 3584 + s * 896:3584 + (s + 1) * 896] for s in range(2)]
            Ssbs = [RSf[:, 2688:2688 + 896]]
            units = []
            if g == 0:
                for sq in range(4):
                    for h in range(8):
                        for qb in range(2):
                            tok = sq * 256 + qb * 128
                            units.append(dict(nq=128, pair=False,
                                              hds=[dict(p0=0, np=128, q=qT[:, h, tok:tok + 128],
                                                        ks=[kT[:, h, 256 + sq * 256:256 + sq * 256 + 256]],
                                                        vs=[Vt[:, 2 + 2 * sq + i, h, :] for i in range(2)], bias2=None)],
                                              out=OT[:, h, tok:tok + 128], bias=None, pad=False, okey=("R2", "OT", tok // 128)))
            else:
                for r in range(16):
                    rs_ = min(max(r - 4, 0), 8)
                    odd = rs_ % 2 == 1
                    c0 = (rs_ - 1) // 2 if odd else rs_ // 2
                    for hp in range(4):
                        dr0 = rs_ - r + 7
                        hds = []
                        for i_ in range(2):
                            h = 2 * hp + i_
                            hds.append(dict(p0=64 * i_, np=64, q=qT[:, h, r * 64:(r + 1) * 64],
                                            ks=[kT[:, h, 256 + rs_ * 64:256 + rs_ * 64 + 512], kT[:, h, 0:256]],
                                            vs=[Vt[:, 2 + c0 + i, h, :] for i in range(5 if odd else 4)] + [Vt[:, i, h, :] for i in range(2)],
                                            bias2=GBX[64 * i_:64 * i_ + 64, hp * 960 + dr0 * 64:hp * 960 + (dr0 + 8) * 64]))
                        units.append(dict(nq=128, pair=True, hds=hds,
                                          out=OT[:, 2 * hp:2 * hp + 2, r * 64:(r + 1) * 64],
                                          bias=Tq[:, hp, dr0:dr0 + 8, :], pad=odd, okey=("R2", "OT", r // 2)))

            def stage1(u, ui):
                sb0 = 2 * (ui % 3)
                u["sb0"] = sb0
                for hd in u["hds"]:
                    p0, np_ = hd["p0"], hd["np"]
                    col = 0
                    for kseg in hd["ks"]:
                        n = kseg.shape[-1]
                        o = 0
                        while o < n:
                            w = min(512, n - o)
                            bsel = sb0 + col // 512
                            fuse = u["bias"] is not None and not u["pad"] and col == 0
                            kw = dict(tile_position=(0, p0)) if p0 else {}
                            op("pe", "matmul", [("R4", "qT", 0), ("R4", "kT", 0)], [pskey(bsel)],
                               out=PS[p0:p0 + np_, sb0 * 512 + col:sb0 * 512 + col + w], lhsT=hd["q"], rhs=kseg[:, o:o + w],
                               start=True, stop=not fuse, **kw)
                            if fuse:
                                kw2 = dict(tile_position=(p0, p0)) if p0 else {}
                                op("pe", "matmul", [("GBX", "Tq"), ("c", "identb")], [pskey(bsel)],
                                   out=PS[p0:p0 + np_, sb0 * 512:sb0 * 512 + 512], lhsT=identb[p0:p0 + 64, p0:p0 + 64],
                                   rhs=hd["bias2"], start=False, stop=True, **kw2)
                            o += w
                            col += w
                    u["lk"] = col

            def stage2a(u, ui):
                nq, lk, sb0 = u["nq"], u["lk"], u["sb0"]
                nb = (lk + 511) // 512
                sk = [pskey(sb0 + i) for i in range(nb)]
                src = PS[0:nq, sb0 * 512:sb0 * 512 + lk]
                skeys = sk
                if u["bias"] is not None and u["pad"]:
                    Ssb = Ssbs[0]
                    ssk_ = ("RS", "Ssb", 0)
                    off = 64 if u["pad"] else 0
                    tot = lk + (128 if u["pad"] else 0)
                    if u["pad"]:
                        op("pool", "memset", [], [ssk_], ap=Ssb[0:nq, 0:64], constant=NEG)
                        op("pool", "memset", [], [ssk_], ap=Ssb[0:nq, 576:640], constant=NEG)
                    op("dve", "tensor_tensor", [sk[0], ("GBX", "Tq")], [ssk_],
                       out=Ssb[0:nq, off:off + 512].rearrange("p (r k) -> p r k", r=8),
                       in0=PS[0:nq, sb0 * 512:sb0 * 512 + 512].rearrange("p (r k) -> p r k", r=8), in1=u["bias"], op=ALU.add)
                    op("act", "copy", [sk[1]], [ssk_], out=Ssb[0:nq, tot - 256:tot],
                       in_=PS[0:nq, sb0 * 512 + 512:sb0 * 512 + 768])
                    src = Ssb[0:nq, 0:tot]
                    skeys = [ssk_]
                    lk = tot
                    u["lk"] = tot
                pi = ui % NPB
                P = Pb[pi][0:nq, 0:lk]
                pk = ("RS", "P", pi)
                nmx, nmk = newstat()
                op("dve", "tensor_reduce", skeys, nmk, out=nmx[0:nq], in_=src, axis=AX.X, op=ALU.max, negate=True)
                rsum, rsk = newstat()
                op("act", "activation", skeys + nmk, [pk] + rsk, out=P, in_=src, func=AF.Exp, bias=nmx[0:nq], scale=1.0,
                   accum_out=rsum[0:nq])
                rinv, rik = newstat()
                op("dve", "reciprocal", rsk, rik, out=rinv[0:nq], in_=rsum[0:nq])
                op("dve", "tensor_scalar", [pk] + rik, [pk], out=P, in0=P, scalar1=rinv[0:nq], scalar2=None, op0=ALU.mult)

            def stage2b(u, ui):
                nq, lk = u["nq"], u["lk"]
                pi = ui % NPB
                pk = ("RS", "P", pi)
                nch = lk // 128
                ps_ = 0 if nch * nq > 512 else ui % 2
                ptb = 6 * 1024 + ps_ * 512
                ptk = ("ps", "PT", ps_)
                for c in range(nch):
                    op("pe", "transpose", [pk, ("c", "identb")], [ptk], out=PSB[:, ptb + c * nq:ptb + (c + 1) * nq],
                       in_=Pb[pi][0:nq, c * 128:(c + 1) * 128], identity=identb[0:nq, 0:nq])
                ptsk = ("RS", "PT", ui % 2)
                PT = PTs[ui % 2][:, 0:nch * nq].rearrange("p (c q) -> p c q", c=nch)
                op("act", "copy", [ptk], [ptsk], out=PT,
                   in_=PSB[:, ptb:ptb + nch * nq].rearrange("p (c q) -> p c q", c=nch))
                ovs = ui % 4
                OV = PS[:, 7 * 512 + ovs * 128:7 * 512 + ovs * 128 + nq]
                for hd in u["hds"]:
                    p0, np_ = hd["p0"], hd["np"]
                    for c in range(nch):
                        op("pe", "matmul", [ptsk, ("R4", "V", 0)], [("ps", "OV", ovs)], out=OV[:, p0:p0 + np_], lhsT=hd["vs"][c],
                           rhs=PT[:, c, p0:p0 + np_], start=(c == 0), stop=(c == nch - 1))
                if u["pair"]:
                    op("act", "copy", [("ps", "OV", ovs)], [u["okey"]], out=u["out"],
                       in_=OV.rearrange("p (i q) -> p i q", i=2)[:, :, ::-1])
                else:
                    op("act", "copy", [("ps", "OV", ovs)], [u["okey"]], out=u["out"], in_=OV)

            nu = len(units)
            for ui in range(nu + 3):
                if ui < nu:
                    stage1(units[ui], ui)
                if 0 <= ui - 1 < nu:
                    stage2a(units[ui - 1], ui - 1)
                if 0 <= ui - 3:
                    stage2b(units[ui - 3], ui - 3)
            S.retire("ps")
            S.retire("R4")
            S.retire("RS")
            S.retire("GBX")
            if stop_after == ("C", g):
                break

            if g == 0:
                load_modcols((2, 3))
            o_sb = R4[:, 0:32768].bitcast(F32).rearrange("p (t d) -> p t d", t=8)
            dma("sp", GB, bass.AP(modrows, (2 * 2 + g) * D, [[0, 128], [1, D]]), [("mod", 2)], [("GBX", "G")], "c10")
            out_blocks = WIDX[("out", g)]
            ssq = {}
            junk = GBX[:, 4096:4096 + 512]
            for j in range(4):
                W, wk = w_get(out_blocks[j])
                for t in range(NT):
                    b = cnt % 8
                    cnt += 1
                    for c in range(16):
                        op("pe", "matmul", [wk, ("R2", "OT", t)], [pskey(b)], out=bank(b), lhsT=OT[:, c, t * 128:(t + 1) * 128],
                           rhs=W[:, c, :], start=(c == 0), stop=(c == 15))
                    if j == 0:
                        ssq[t] = newstat(4)
                    ss4, ss4k = ssq[t]
                    op("dve", "tensor_tensor", [pskey(b), ("GBX", "G")], [("R4", "o", t)], out=o_sb[:, t, j * 512:(j + 1) * 512],
                       in0=bank(b), in1=GB[:, j * 512:(j + 1) * 512], op=ALU.mult)
                    op("act", "activation", [("R4", "o", t), pskey(b)], [("GBX", "junk"), ss4k[j]], out=junk,
                       in_=bank(b), func=AF.Square, accum_out=ss4[:, j:j + 1])
                w_done(out_blocks[j])
            S.retire("R2")
            if stop_after == ("D1", g):
                break
            def dpost_stages(t):
                sl = t % 2
                st_ = {}

                def s0():
                    dma("sp", xt[sl], x_d[t * 128:(t + 1) * 128, :], [], [("RS", "x", sl)], "x%d" % sl)

                def s1():
                    ss4, ss4k = ssq[t]
                    st_["ss"] = newstat()
                    op("dve", "tensor_reduce", ss4k, st_["ss"][1], out=st_["ss"][0], in_=ss4, axis=AX.X, op=ALU.add)

                def s2():
                    st_["sd"] = newstat()
                    op("act", "activation", st_["ss"][1] + [("c", "eps")], st_["sd"][1], out=st_["sd"][0], in_=st_["ss"][0],
                       func=AF.Sqrt, bias=epst[:, 0:1], scale=1.0 / D)

                def s3():
                    st_["rs"] = newstat()
                    op("dve", "reciprocal", st_["sd"][1], st_["rs"][1], out=st_["rs"][0], in_=st_["sd"][0])

                def s4():
                    op("dve", "scalar_tensor_tensor", [("R4", "o", t), ("RS", "x", sl)] + st_["rs"][1], [("R4", "o", t)],
                       out=o_sb[:, t, :], in0=o_sb[:, t, :], scalar=st_["rs"][0], in1=xt[sl], op0=ALU.mult, op1=ALU.add)

                def s5():
                    dma("sp", y_d[t * 128:(t + 1) * 128, :], o_sb[:, t, :], [("R4", "o", t)], [("y", g, t)], "ys%d" % t)
                    st_["s2"] = newstat()
                    op("act", "activation", [("R4", "o", t)], [("GBX", "xn", sl)] + st_["s2"][1], out=xn[sl], in_=o_sb[:, t, :],
                       func=AF.Square, accum_out=st_["s2"][0])

                def s6():
                    st_["d2"] = newstat()
                    op("act", "activation", st_["s2"][1] + [("c", "eps")], st_["d2"][1], out=st_["d2"][0], in_=st_["s2"][0],
                       func=AF.Sqrt, bias=epst[:, 0:1], scale=1.0 / D)

                def s7():
                    st_["r2"] = newstat()
                    op("dve", "reciprocal", st_["d2"][1], st_["r2"][1], out=st_["r2"][0], in_=st_["d2"][0])

                def s8():
                    op("dve", "tensor_scalar", [("R4", "o", t)] + st_["r2"][1], [("GBX", "xn", sl)], out=xn[sl], in0=o_sb[:, t, :],
                       scalar1=st_["r2"][0], scalar2=None, op0=ALU.mult)

                def back():
                    _norm_back(t, 2, 3, sl, 3 * sl, 12)
                return [s0, s1, s2, s3, s4, s5, s6, s7, s8], back

            prev_backs = []
            dpairs = [[dpost_stages(t0_), dpost_stages(t0_ + 1)] for t0_ in range(0, NT, 2)]
            for pi_, pair in enumerate(dpairs):
                for si in range(9):
                    if si == 0 and pi_ > 0:
                        continue
                    for stg_, _ in pair:
                        stg_[si]()
                    if si == 4:
                        if pi_ + 1 < len(dpairs):
                            for stg_, _ in dpairs[pi_ + 1]:
                                stg_[0]()
                        for bk in prev_backs:
                            bk()
                        prev_backs = []
                prev_backs = [bk for _, bk in pair]
            for bk in prev_backs:
                bk()
            S.retire("R4")
            S.retire("RS")
            S.retire("GBX")
            if stop_after == ("D", g):
                break

            f_sb = R4[:, 0:32768].bitcast(F32).rearrange("p (t d) -> p t d", t=8)
            gT = R4[:, 32768:32768 + 11264].rearrange("p (k t) -> p k t", k=11)
            dma("sp", GB, bass.AP(modrows, (5 * 2 + g) * D, [[0, 128], [1, D]]), [("mod", 5)], [("GBX", "G")], "c10")
            nseq = 4 if g == 0 else 1
            sl_len = 1024 // nseq
            for qd in range(4):
                for (c0, ncz) in EGROUPS:
                    i0 = 11 * qd + c0
                    bv = WIDX[("up", g, qd, c0)]
                    Wvg, wvk = w_get(bv)
                    wgk = wvk
                    Wv = Wvg[:, :, 0:256]
                    Wg = Wvg[:, :, 256:512]
                    for ci in range(ncz):
                        i = i0 + ci
                        il = c0 + ci
                        pb0 = 4 * (cnt % 2)
                        cnt += 1
                        ek = cnt % 2
                        for c in range(16):
                            for half, (Wx, wxk) in enumerate(((Wv, wvk), (Wg, wgk))):
                                for tb in range(2):
                                    b = pb0 + half * 2 + tb
                                    op("pe", "matmul", [wxk] + [("R2", "hT", tb * 4 + k) for k in range(4)], [pskey(b)],
                                       out=bank(b), lhsT=Wx[:, c, ci * 128:(ci + 1) * 128], rhs=hT[:, c, tb * 512:(tb + 1) * 512],
                                       start=(c == 0), stop=(c == 15))
                        acc = []
                        for half in range(2):
                            ch = i if half == 0 else 44 + i
                            a = RSf[:, ek * 2048 + half * 1024:ek * 2048 + (half + 1) * 1024]
                            ak = ("RS", "acc", ek, half)
                            acc.append((a, ak))
                            for tb in range(2):
                                b = pb0 + half * 2 + tb
                                op("act", "activation", [pskey(b), ("c", "taps")], [ak], out=a[:, tb * 512:(tb + 1) * 512],
                                   in_=bank(b), func=AF.Identity, scale=taps[:, 1, ch:ch + 1], bias=taps[:, 3, ch:ch + 1])
                            for tb in range(2):
                                b = pb0 + half * 2 + tb
                                nsb = 512 // sl_len if sl_len < 512 else 1
                                ln = min(512, sl_len)
                                a3 = a[:, tb * 512:(tb + 1) * 512].rearrange("p (s l) -> p s l", s=nsb)
                                p3 = bank(b).rearrange("p (s l) -> p s l", s=nsb)
                                op("dve", "scalar_tensor_tensor", [pskey(b), ("c", "taps"), ak], [ak], out=a3[:, :, 1:ln],
                                   in0=p3[:, :, 0:ln - 1], scalar=taps[:, 0, ch:ch + 1], in1=a3[:, :, 1:ln], op0=ALU.mult, op1=ALU.add)
                                op("dve", "scalar_tensor_tensor", [pskey(b), ("c", "taps"), ak], [ak], out=a3[:, :, 0:ln - 1],
                                   in0=p3[:, :, 1:ln], scalar=taps[:, 2, ch:ch + 1], in1=a3[:, :, 0:ln - 1], op0=ALU.mult, op1=ALU.add)
                            if sl_len > 512:
                                b0_, b1_ = pb0 + half * 2, pb0 + half * 2 + 1
                                op("dve", "scalar_tensor_tensor", [pskey(b0_), ("c", "taps"), ak], [ak], out=a[:, 512:513],
                                   in0=bank(b0_)[:, 511:512], scalar=taps[:, 0, ch:ch + 1], in1=a[:, 512:513], op0=ALU.mult, op1=ALU.add)
                                op("dve", "scalar_tensor_tensor", [pskey(b1_), ("c", "taps"), ak], [ak], out=a[:, 511:512],
                                   in0=bank(b1_)[:, 0:1], scalar=taps[:, 2, ch:ch + 1], in1=a[:, 511:512], op0=ALU.mult, op1=ALU.add)
                        (av, avk), (ag, agk) = acc
                        sg = GBX[:, 4096 + ek * 2048:4096 + (ek + 1) * 2048].bitcast(F32) if False else None
                        sgt = xn[ek].bitcast(F32) if False else None
                        op("act", "activation", [agk], [agk], out=ag, in_=ag, func=AF.Silu)
                        op("dve", "tensor_tensor", [agk, avk], [("R4", "gT", il)], out=gT[:, il, :], in0=ag, in1=av, op=ALU.mult)
                    w_done(bv)
                for j in range(4):
                    bd = WIDX[("dn", g, qd)][j]
                    Wd, wdk = w_get(bd)
                    fbanks = {}
                    if j == 0:
                        for t in range(NT):
                            fbanks[t] = cnt % 8
                            cnt += 1
                            for k in range(10):
                                op("pe", "matmul", [wdk, ("R4", "gT", k)], [pskey(fbanks[t])], out=bank(fbanks[t]),
                                   lhsT=gT[:, k, t * 128:(t + 1) * 128], rhs=Wd[:, k, :], start=(k == 0), stop=False)
                    for t in range(NT):
                        if j == 0:
                            b = fbanks[t]
                            krange = range(10, 11)
                        else:
                            b = cnt % 8
                            cnt += 1
                            krange = range(11)
                        for k in krange:
                            op("pe", "matmul", [wdk, ("R4", "gT", k)], [pskey(b)], out=bank(b), lhsT=gT[:, k, t * 128:(t + 1) * 128],
                               rhs=Wd[:, k, :], start=(k == 0), stop=(k == 10))
                        fo = f_sb[:, t, j * 512:(j + 1) * 512]
                        if qd == 0:
                            op("act", "copy", [pskey(b)], [("R4", "f", t)], out=fo, in_=bank(b))
                        else:
                            op("dve", "tensor_tensor", [pskey(b), ("R4", "f", t)], [("R4", "f", t)], out=fo, in0=bank(b), in1=fo,
                               op=ALU.add)
                    w_done(bd)
            S.retire("RS")
            S.retire("R2")
            dma("sp", xt[0], y_d[0:128, :], [("y", g, 0)], [("RS", "x", 0)], "x0")
            for t in range(NT):
                sl = t % 2
                if t + 1 < NT:
                    dma("sp", xt[1 - sl], y_d[(t + 1) * 128:(t + 2) * 128, :], [("y", g, t + 1)], [("RS", "x", 1 - sl)], "x%d" % (1 - sl))
                ss, ssk = newstat()
                op("act", "activation", [("R4", "f", t)], [("GBX", "xn", sl)] + ssk, out=xn[sl], in_=f_sb[:, t, :], func=AF.Square,
                   accum_out=ss)
                rs, rsk = rstd_from_ss(ss, ssk, 1, 1.0 / D)
                op("dve", "scalar_tensor_tensor", [("R4", "f", t), ("GBX", "G")] + rsk, [("R4", "f", t)], out=f_sb[:, t, :],
                   in0=f_sb[:, t, :], scalar=rs, in1=GB, op0=ALU.mult, op1=ALU.mult)
                op("dve", "tensor_tensor", [("R4", "f", t), ("RS", "x", sl)], [("R4", "f", t)], out=f_sb[:, t, :], in0=f_sb[:, t, :],
                   in1=xt[sl], op=ALU.add)
                dma("sp", y_d[t * 128:(t + 1) * 128, :], f_sb[:, t, :], [("R4", "f", t)], [("y", g, t)], "xo%d" % sl)
            S.retire("R4")
            S.retire("RS")
            S.retire("GBX")

        sems = {e: es.enter_context(nc.semaphore("s_" + e)) for e in COMPUTE}
        dsems = {k: es.enter_context(nc.semaphore("d_" + k)) for k in dsem_keys}
        with nc.Block() as block:
            S.emit_block(block, sems, dsems)
    return nc


def _consts():
    half = 64
    freqs = (10000.0 ** (-np.arange(0, half, 2, dtype=np.float32) / half)).astype(np.float32)
    t = np.arange(T)
    ar = (t // 64).astype(np.float32)[:, None] * freqs[None, :]
    ac = (t % 64).astype(np.float32)[:, None] * freqs[None, :]
    C = np.concatenate([np.cos(ar), np.cos(ar), np.cos(ac), np.cos(ac)], axis=1).astype(np.float32)
    Sn = np.concatenate([-np.sin(ar), np.sin(ar), -np.sin(ac), np.sin(ac)], axis=1).astype(np.float32)
    col = np.arange(64)
    cs = np.clip(col - 8, 0, 48)
    valid = (col[None, :] >= cs[:, None]) & (col[None, :] < cs[:, None] + 16)
    mask = np.where(valid, 0.0, NEG).astype(np.float32)[::-1].copy()
    return C, Sn, mask, np.eye(128, dtype=np.float32)


_PROG = {}


def kernel(x_prompt, x_sample, c, cache_a_k, cache_a_v, cache_b_k, cache_b_v, c_ctx, w_mod, b_mod,
           g_attn_pre, g_attn_post, g_ffn_pre, g_ffn_post, w_in, rpb, g_qnorm, g_knorm, w_out, w_up,
           conv_w, conv_b, w_down, _stop_after=None):
    f = lambda a: np.ascontiguousarray(np.asarray(a, dtype=np.float32))
    key = _stop_after
    if key not in _PROG:
        _PROG[key] = build_program(_stop_after)
    nc = _PROG[key]
    C, Sn, mask, ident = _consts()
    x_prompt, x_sample, c = f(x_prompt), f(x_sample), f(c)
    shared = {
        "w_mod": f(w_mod)[0], "b_mod": f(b_mod), "g_attn_pre": f(g_attn_pre), "g_attn_post": f(g_attn_post),
        "g_ffn_pre": f(g_ffn_pre), "g_ffn_post": f(g_ffn_post), "w_in": f(w_in)[0],
        "rpbpad": np.pad(f(rpb).reshape(-1), 64)[None, :].copy(), "g_qnorm": f(g_qnorm), "g_knorm": f(g_knorm),
        "w_out": f(w_out)[0], "w_up": f(w_up)[0], "conv_w": f(conv_w)[0], "conv_b": f(conv_b), "w_down": f(w_down)[0],
        "ident": ident, "ropeC": C, "ropeS": Sn, "maskrev": mask,
    }
    in_maps = []
    for i in range(8):
        m = dict(shared)
        m["xp"] = x_prompt[4 * i:4 * i + 4].reshape(T, D)
        m["xs"] = x_sample[i]
        m["cond"] = np.stack([f(c_ctx), c[i]], axis=0)
        m["cak"] = f(cache_a_k)[i, 0]
        m["cav"] = f(cache_a_v)[i, 0]
        m["cbk"] = f(cache_b_k)[i, 0]
        m["cbv"] = f(cache_b_v)[i, 0]
        in_maps.append(m)
    res = run_bass_kernel_spmd(nc, in_maps, core_ids=list(range(8)))
    R = res.results
    yp = np.concatenate([R[i]["yp"].reshape(4, 256, D) for i in range(8)], axis=0)
    ys = np.stack([R[i]["ys"] for i in range(8)], axis=0)
    outs = [yp, ys]
    for n in ("sak", "sav", "sbk", "sbv"):
        outs.append(np.concatenate([R[i][n] for i in range(8)], axis=0)[:, None])
    return tuple(np.ascontiguousarray(o, dtype=np.float32) for o in outs)
```
